# Optimizing a Trainium2 kernel written in Bass

```python
import jax, jax.numpy as jnp
from jax import lax
import numpy as np

D_MODEL = 4096
BATCH = 4
SEQ = 2048
DEPTH = 4
DEC_BATCH = 128
DEC_SEQ = 4
PAST_LEN = 16384
PAGE_SIZE = 128

N_MIXERS = 2
POOL_WINDOWS = (2, 4, 8, 16)
N_POOL_GROUPS = len(POOL_WINDOWS)
POOL_GROUP_DIM = D_MODEL // N_POOL_GROUPS
POOL_STATE_LEN = max(POOL_WINDOWS) - 1
CONV_WIDTH = 3
CONV_STATE_LEN = CONV_WIDTH - 1
D_FF = 4 * D_MODEL
N_POOL_LAYERS = (DEPTH + 1) // 2
N_CONV_LAYERS = DEPTH // 2
RMS_EPS = 1e-6

kernel_name = "interleaved_pool_shortconv_decoder_step"


def rmsnorm(x, g):
    xf = x.astype(jnp.float32)
    inv = lax.rsqrt(jnp.mean(xf * xf, axis=-1, keepdims=True) + RMS_EPS)
    return (xf * inv).astype(x.dtype) * g


def pool_mixer(u, prev, start, w_pool, pool_scale):
    b, t, d = u.shape
    p = POOL_STATE_LEN
    padded = jnp.concatenate([prev.astype(u.dtype), u], axis=1)
    cs = jnp.cumsum(padded.astype(jnp.float32), axis=1)
    cs = jnp.pad(cs, ((0, 0), (1, 0), (0, 0)))
    end = cs[:, p + 1:]
    pos = start + jnp.arange(t, dtype=jnp.int32)
    pooled = []
    for g, w in enumerate(POOL_WINDOWS):
        sl = slice(g * POOL_GROUP_DIM, (g + 1) * POOL_GROUP_DIM)
        s = end[..., sl] - cs[:, p + 1 - w: p + 1 - w + t, sl]
        cnt = jnp.minimum(pos + 1, w).astype(jnp.float32)[None, :, None]
        pooled.append(s / cnt)
    pooled = jnp.concatenate(pooled, axis=-1)
    diff = (pooled - u.astype(jnp.float32)).astype(u.dtype)
    diff = diff.reshape(b, t, N_POOL_GROUPS, POOL_GROUP_DIM)
    y = jnp.einsum("btgc,gcd->btgd", diff, w_pool).reshape(b, t, d) * pool_scale
    new_prev = padded[:, -p:]
    return y, new_prev


def conv_mixer(u, prev, w_in, conv_w, w_out):
    t = u.shape[1]
    z = jnp.einsum("btd,de->bte", u, w_in)
    gate_b, gate_c, h = jnp.split(z, 3, axis=-1)
    v = gate_c * h
    padded = jnp.concatenate([prev.astype(v.dtype), v], axis=1)
    conv = conv_w[0] * padded[:, 0:t]
    for k in range(1, CONV_WIDTH):
        conv = conv + conv_w[k] * padded[:, k:k + t]
    y = jnp.einsum("btd,de->bte", gate_b * conv, w_out)
    new_prev = padded[:, -CONV_STATE_LEN:]
    return y, new_prev


def trunk(x, start, pool_state, conv_state, norm_mix, norm_mlp, norm_final,
          w_pool, pool_scale, w_conv_in, conv_w, w_conv_out, w_up, w_down):
    new_pool, new_conv = [], []
    for i in range(DEPTH):
        u = rmsnorm(x, norm_mix[i])
        j = i // N_MIXERS
        if i % N_MIXERS == 0:
            m, st = pool_mixer(u, pool_state[j], start, w_pool[j], pool_scale[j])
            new_pool.append(st)
        else:
            m, st = conv_mixer(u, conv_state[j], w_conv_in[j], conv_w[j], w_conv_out[j])
            new_conv.append(st)
        x = x + m
        u = rmsnorm(x, norm_mlp[i])
        hdn = jnp.square(jax.nn.relu(jnp.einsum("btd,df->btf", u, w_up[i])))
        x = x + jnp.einsum("btf,fd->btd", hdn, w_down[i])
    y = rmsnorm(x, norm_final)
    return y, jnp.stack(new_pool), jnp.stack(new_conv)


def setup_inputs(seed: int = 0) -> dict:
    key = jax.random.key(seed)
    ks = jax.random.split(key, 16)
    f32 = jnp.float32
    nrm = jax.random.normal
    x_prompt = nrm(ks[0], (BATCH, SEQ, D_MODEL), f32)
    x_sample = nrm(ks[1], (DEC_BATCH, DEC_SEQ, D_MODEL), f32)
    state_pool = nrm(ks[2], (N_POOL_LAYERS, DEC_BATCH, POOL_STATE_LEN, D_MODEL), f32)
    state_conv = nrm(ks[3], (N_CONV_LAYERS, DEC_BATCH, CONV_STATE_LEN, D_MODEL), f32)
    norm_mix = 1.0 + 0.02 * nrm(ks[4], (DEPTH, D_MODEL), f32)
    norm_mlp = 1.0 + 0.02 * nrm(ks[5], (DEPTH, D_MODEL), f32)
    norm_final = 1.0 + 0.02 * nrm(ks[6], (D_MODEL,), f32)
    w_pool = nrm(ks[7], (N_POOL_LAYERS, N_POOL_GROUPS, POOL_GROUP_DIM, POOL_GROUP_DIM), f32) * POOL_GROUP_DIM ** -0.5
    pool_scale = 1.0 + 0.1 * nrm(ks[8], (N_POOL_LAYERS, D_MODEL), f32)
    w_conv_in = nrm(ks[9], (N_CONV_LAYERS, D_MODEL, 3 * D_MODEL), f32) * D_MODEL ** -0.5
    conv_w = nrm(ks[10], (N_CONV_LAYERS, CONV_WIDTH, D_MODEL), f32) * CONV_WIDTH ** -0.5
    w_conv_out = nrm(ks[11], (N_CONV_LAYERS, D_MODEL, D_MODEL), f32) * D_MODEL ** -0.5
    w_up = nrm(ks[12], (DEPTH, D_MODEL, D_FF), f32) * D_MODEL ** -0.5
    w_down = nrm(ks[13], (DEPTH, D_FF, D_MODEL), f32) * (0.5 * D_FF ** -0.5)
    return {"x_prompt": x_prompt, "x_sample": x_sample,
            "state_pool": state_pool, "state_conv": state_conv,
            "norm_mix": norm_mix, "norm_mlp": norm_mlp, "norm_final": norm_final,
            "w_pool": w_pool, "pool_scale": pool_scale,
            "w_conv_in": w_conv_in, "conv_w": conv_w, "w_conv_out": w_conv_out,
            "w_up": w_up, "w_down": w_down}


def reference(x_prompt, x_sample, state_pool, state_conv, norm_mix, norm_mlp, norm_final,
              w_pool, pool_scale, w_conv_in, conv_w, w_conv_out, w_up, w_down):
    b = x_prompt.shape[0]
    zero_pool = jnp.zeros((N_POOL_LAYERS, b, POOL_STATE_LEN, D_MODEL), x_prompt.dtype)
    zero_conv = jnp.zeros((N_CONV_LAYERS, b, CONV_STATE_LEN, D_MODEL), x_prompt.dtype)
    y_prompt, new_pool_prompt, new_conv_prompt = trunk(
        x_prompt, 0, zero_pool, zero_conv, norm_mix, norm_mlp, norm_final,
        w_pool, pool_scale, w_conv_in, conv_w, w_conv_out, w_up, w_down)
    y_sample, new_pool_sample, new_conv_sample = trunk(
        x_sample, PAST_LEN, state_pool, state_conv, norm_mix, norm_mlp, norm_final,
        w_pool, pool_scale, w_conv_in, conv_w, w_conv_out, w_up, w_down)
    return (y_prompt, y_sample, new_pool_prompt, new_pool_sample, new_conv_prompt, new_conv_sample)
```

```python
import numpy as np
import concourse.bass as bass
import concourse.mybir as mybir
from concourse.bass_utils import run_bass_kernel_spmd

F32 = mybir.dt.float32
BF16 = mybir.dt.bfloat16
AF = mybir.ActivationFunctionType
ALU = mybir.AluOpType

D = 4096
NCH = 32
DFF = 16384
DEPTH = 4
HALO = 34
NSP = 32
NPASS = 2
NTOK = (446, 578)
PCS = (HALO + 446, 578)
TS = (PCS[0] + NSP, PCS[1] + NSP)
SPLS = (((0, 512),), ((0, 512), (512, 610)))
TMAX = max(TS)
PAD = 16
SW = PAD + TMAX
WINS = (2, 4, 8, 16)
RING_UNITS = 14
UNIT = 4096
EPS = 1e-6
N_WSEM = 12
N_IOSEM = 8


class Op:
    __slots__ = ("eng", "fn", "deps", "mark", "seq", "dma", "sem_idx", "sem_val", "uid")

    def __init__(self, eng, fn, dma, uid):
        self.eng = eng
        self.fn = fn
        self.dma = dma
        self.deps = []
        self.mark = False
        self.seq = 0
        self.sem_idx = 0
        self.sem_val = 0
        self.uid = uid


class _Rec:
    def __getattr__(self, name):
        def call(*a, **kw):
            self.call = (name, a, kw)
            return None
        return call


class Sched:
    ENGS = ("pe", "act", "dve", "pool", "sp")

    def __init__(self):
        self.ops = {e: [] for e in self.ENGS}
        self.last_w = {}
        self.readers = {}
        self.n = 0

    def add(self, eng, fn, r=(), w=(), dma=False):
        self.n += 1
        rec = _Rec()
        fn(rec)
        op = Op(eng, rec.call, dma, self.n)
        deps = op.deps
        seen = set()
        is_pe = (eng == "pe")

        def dep(p):
            if p is None or p.uid in seen:
                return
            if is_pe and p.eng == "pe":
                return
            seen.add(p.uid)
            deps.append(p)
            p.mark = True

        lw = self.last_w
        rdm = self.readers
        for k in r:
            dep(lw.get(k))
        for k in w:
            dep(lw.get(k))
            rd = rdm.get(k)
            if rd:
                for p in rd.values():
                    dep(p)
        for k in w:
            lw[k] = op
            rdm[k] = {}
        rk = ("dma", op.uid) if dma else eng
        for k in r:
            d = rdm.get(k)
            if d is None:
                d = rdm[k] = {}
            d[rk] = op
        self.ops[eng].append(op)
        return op

    def finalize(self):
        for e in ("pe", "act", "dve"):
            c = 0
            for op in self.ops[e]:
                if op.mark:
                    c += 1
                    op.seq = c
        self.dma_final = {}
        for q, nsem in (("pool", N_WSEM), ("sp", N_IOSEM)):
            c = 0
            for op in self.ops[q]:
                assert op.dma
                op.sem_idx = c % nsem
                op.sem_val = 16 * (c // nsem + 1)
                self.dma_final[(q, op.sem_idx)] = op.sem_val
                c += 1

    def emit(self, nc, block, sems):
        def token(p):
            if p.dma:
                return (p.eng, p.sem_idx), p.sem_val
            return p.eng, p.seq

        def run(e, name):
            known = {}
            for op in self.ops[name]:
                need = {}
                for p in op.deps:
                    k, v = token(p)
                    if known.get(k, 0) < v and need.get(k, 0) < v:
                        need[k] = v
                if op.dma and op.sem_val > 16:
                    k = (name, op.sem_idx)
                    v = op.sem_val - 16
                    if known.get(k, 0) < v and need.get(k, 0) < v:
                        need[k] = v
                for k, v in need.items():
                    e.wait_ge(sems[k], v)
                    known[k] = v
                mname, a_, kw_ = op.fn
                ins = getattr(e, mname)(*a_, **kw_)
                if op.dma:
                    ins.then_inc(sems[(name, op.sem_idx)], 16)
                elif op.mark:
                    ins.then_inc(sems[name], 1)
            if name == "sp":
                for (q, i), v in self.dma_final.items():
                    if q == "sp" and known.get((q, i), 0) < v:
                        e.wait_ge(sems[(q, i)], v)

        @block.tensor
        def _(e):
            run(e, "pe")

        @block.scalar
        def _(e):
            run(e, "act")

        @block.vector
        def _(e):
            run(e, "dve")

        @block.gpsimd
        def _(e):
            run(e, "pool")

        @block.sync
        def _(e):
            run(e, "sp")


class Tile:
    __slots__ = ("key", "kind", "size", "start", "issue", "issued", "keys")

    def __init__(self, key, kind, size, issue):
        self.key = key
        self.kind = kind
        self.size = size
        self.issue = issue
        self.issued = False
        self.start = -1
        self.keys = ()


class Ring:
    def __init__(self, n):
        self.n = n
        self.occ = [False] * n
        self.head = 0
        self.plan = []
        self.nxt = 0
        self.cons = 0

    def pump(self):
        while self.nxt < len(self.plan):
            t = self.plan[self.nxt]
            s = -1
            for d in range(self.n):
                c = (self.head + d) % self.n
                if c + t.size <= self.n and not any(self.occ[c:c + t.size]):
                    s = c
                    break
            if s < 0:
                return
            for u in range(s, s + t.size):
                self.occ[u] = True
            self.head = (s + t.size) % self.n
            t.start = s
            t.keys = tuple(("r", u) for u in range(s, s + t.size))
            t.issued = True
            self.nxt += 1
            if t.issue is not None:
                t.issue(t)

    def next(self, key):
        t = self.plan[self.cons]
        assert t.key == key, (t.key, key)
        if not t.issued:
            self.pump()
        assert t.issued, ("ring deadlock", key)
        self.cons += 1
        return t

    def free(self, t):
        for u in range(t.start, t.start + t.size):
            self.occ[u] = False
        self.pump()


def build_nc():
    nc = bass.Bass("TRN2", target_bir_lowering=False)

    def din(name, shape):
        return nc.dram_tensor(name, list(shape), F32, kind="ExternalInput").ap()

    def dout(name, shape):
        return nc.dram_tensor(name, list(shape), F32, kind="ExternalOutput").ap()

    xp = din("xp", (HALO + 1024, D))
    xs = din("xs", (64, D))
    stp = din("stp", (2, 16, 15, D))
    stc = din("stc", (2, 16, 2, D))
    pos = din("pos", (NPASS, TMAX))
    gmix_d = din("gmix", (128, DEPTH * NCH))
    gmlp_d = din("gmlp", (128, DEPTH * NCH))
    gfin_d = din("gfin", (128, NCH))
    psc_d = din("pscale", (128, 2 * NCH))
    cw_d = din("convw", (128, 2 * 3 * NCH))
    idn_d = din("ident", (128, 128))
    w_pool = din("w_pool", (2, 4, 1024, 1024))
    w_cin = din("w_conv_in", (2 * 3 * 8 * 2 * 128, 32 * 256))
    w_cout = din("w_conv_out", (2 * 8 * 4 * 128, 4 * 1024))
    w_up = din("w_up", (DEPTH * 32 * 2 * 128, 32 * 256))
    w_down = din("w_down", (DEPTH * 32 * 4 * 128, 4 * 1024))

    yp = dout("yp", (1024, D))
    ys = dout("ys", (64, D))
    pool_p = dout("pool_p", (2, 15, D))
    pool_sn = dout("pool_sn", (2, NPASS * NSP, D))
    pool_so = dout("pool_so", (2, 16, 11, D))
    conv_p = dout("conv_p", (2, 2, D))
    conv_s = dout("conv_s", (2, 32, D))

    S = Sched()
    R = Ring(RING_UNITS)

    off = [0]

    def carve(nbytes):
        o = off[0]
        off[0] = o + ((nbytes + 31) // 32) * 32
        return o

    o_x = carve(NCH * TMAX * 4)
    o_u = carve(NCH * TMAX * 2)
    o_h = carve(4 * TMAX * 2)
    o_scr = carve(5 * SW * 4)
    o_rstd = carve(TMAX * 4)
    o_rstd2 = carve(TMAX * 4)
    o_sq = carve(2 * TMAX * 2)
    o_psv = carve(2 * NCH * 15 * 4)
    o_csv = carve(2 * NCH * 2 * 4)
    o_ost = carve(2 * 256 * 4)
    o_tl = carve(NCH * 16 * 4)
    o_vtl = carve(8 * 6 * 4)
    o_vsv = carve(2 * 18 * 4)
    o_ones = carve(128 * 2)
    o_idn = carve(128 * 4)
    o_gmix = carve(DEPTH * NCH * 4)
    o_gmlp = carve(DEPTH * NCH * 4)
    o_gfin = carve(NCH * 4)
    o_psc = carve(2 * NCH * 4)
    o_cw = carve(6 * NCH * 4)
    o_eps = carve(32)
    o_ring = carve(RING_UNITS * UNIT)
    total = off[0]
    assert total <= 212800, total
    print("[kernel] sbuf bytes/partition:", total, "ring units:", RING_UNITS, "x", UNIT)

    ctx_arena = nc.sbuf_tensor("arena", [128, total // 4], F32)
    ctx_ps = nc.psum_tensor("ps", [128, 4, 1024], F32)
    arena = ctx_arena.__enter__()
    pst = ctx_ps.__enter__()

    def vf(o, n):
        return arena[:, o // 4:o // 4 + n]

    def vb(o, n):
        return arena[:, o // 4:o // 4 + n // 2].bitcast(BF16)

    x_t = vf(o_x, NCH * TMAX).rearrange("p (c t) -> p c t", c=NCH)
    u_t = vb(o_u, NCH * TMAX).rearrange("p (c t) -> p c t", c=NCH)
    h_t = vb(o_h, 4 * TMAX).rearrange("p (b j t) -> p b j t", b=1, j=4)
    scr_t = vf(o_scr, 5 * SW).rearrange("p (s t) -> p s t", s=5)
    rstd_f = vf(o_rstd, TMAX)
    rstd2_f = vf(o_rstd2, TMAX)
    sq_t = vb(o_sq, 2 * TMAX).rearrange("p (s t) -> p s t", s=2)
    psv_t = vf(o_psv, 2 * NCH * 15).rearrange("p (j c r) -> p j c r", j=2, c=NCH)
    csv_t = vf(o_csv, 2 * NCH * 2).rearrange("p (j c r) -> p j c r", j=2, c=NCH)

    class P:
        T = TS[0]
        PC = PCS[0]
        SPL = SPLS[0]
    pos1_f = rstd2_f
    ost_t = vf(o_ost, 512).rearrange("p (s t) -> p s t", s=2)
    tl_t = vf(o_tl, 480).rearrange("p (s b t) -> p s b t", s=3, b=8)
    vtl_t = vf(o_vtl, 48).rearrange("p (b t) -> p b t", b=8)
    vsv_t = vf(o_vsv, 36).rearrange("p (s t) -> p s t", s=2)
    vst_t = vf(o_tl, NCH * 16).rearrange("p (c r) -> p c r", c=NCH)
    TLK = [("tl", 0), ("tl", 1), ("tl", 2)]
    ones_t = vb(o_ones, 128)
    idn_t = vf(o_idn, 128)
    gmix_t = vf(o_gmix, DEPTH * NCH)
    gmlp_t = vf(o_gmlp, DEPTH * NCH)
    gfin_t = vf(o_gfin, NCH)
    psc_t = vf(o_psc, 2 * NCH)
    cw_t = vf(o_cw, 6 * NCH)
    eps_t = vf(o_eps, 1)

    def ring_f32(t, n):
        o = o_ring + t.start * UNIT
        return vf(o, n)

    def ring_bf(t, n):
        o = o_ring + t.start * UNIT
        return vb(o, n)

    def ps(s):
        return pst[:, s, :]

    ps_ctr = [0]

    def next_ps():
        s = ps_ctr[0] % 4
        ps_ctr[0] += 1
        return s

    scr_ctr = [0]

    for name, dst, src in (("idn", idn_t, idn_d), ("gmix", gmix_t, gmix_d), ("gmlp", gmlp_t, gmlp_d),
                           ("gfin", gfin_t, gfin_d), ("psc", psc_t, psc_d), ("cw", cw_t, cw_d)):
        S.add("sp", lambda e, dst=dst, src=src: e.dma_start(out=dst, in_=src[:, :]), w=[("c", name)], dma=True)
    S.add("dve", lambda e: e.memset(ones_t, 1.0), w=[("c", "ones")])
    S.add("dve", lambda e: e.memset(eps_t, EPS), w=[("c", "eps")])
    S.add("dve", lambda e: e.memset(vf(o_scr, 5 * SW), 0.0), w=[("scr", i) for i in range(5)])
    S.add("dve", lambda e: e.memset(vf(o_tl, 512), 0.0), w=[("tl", i) for i in range(3)])

    def issue_w(src_fn, shape):
        a, b = shape

        def issue(t):
            dst = ring_bf(t, a * b).rearrange("p (a b) -> p a b", a=a)
            S.add("pool", lambda e: e.dma_start(out=dst, in_=src_fn()), w=t.keys, dma=True)
        return issue

    def plan_w(key, src_fn, shape):
        a, b = shape
        size = (a * b * 2 + UNIT - 1) // UNIT
        R.plan.append(Tile(key, "w", size, issue_w(src_fn, shape)))

    def plan_in(key, src_fn, nrows):
        def issue(t):
            dst = ring_f32(t, D)[0:nrows, :]
            S.add("sp", lambda e: e.dma_start(out=dst, in_=src_fn()), w=t.keys, dma=True)
        R.plan.append(Tile(key, "in", (D * 4) // UNIT, issue))

    def plan_out(key):
        R.plan.append(Tile(key, "out", (D * 4) // UNIT, None))

    def wview(w2d):
        return w2d.rearrange("(kc p) m -> p kc m", p=128)

    def row_blocks(n):
        return [(r0, min(128, n - r0)) for r0 in range(0, n, 128)]

    def in_blocks(p):
        base = 0 if p == 0 else PCS[0]
        blks = []
        for r0, nr in row_blocks(PCS[p]):
            blks.append((lambda r0=r0, nr=nr: xp[base + r0:base + r0 + nr, :], nr, r0))
        blks.append((lambda: xs[NSP * p:NSP * (p + 1), :], NSP, PCS[p]))
        return blks

    def out_blocks(p):
        tok0 = 0 if p == 0 else NTOK[0]
        col0 = HALO if p == 0 else 0
        blks = []
        for r0, nr in row_blocks(NTOK[p]):
            blks.append((yp[tok0 + r0:tok0 + r0 + nr, :], nr, col0 + r0))
        blks.append((ys[NSP * p:NSP * (p + 1), :], NSP, PCS[p]))
        return blks

    for p in range(NPASS):
        for bi, (src_fn, nr, c0) in enumerate(in_blocks(p)):
            plan_in(("xin", p, bi), src_fn, nr)
        for i in range(DEPTH):
            j = i // 2
            if i % 2 == 0:
                plan_in(("stp", p, j),
                        lambda j=j, p=p: stp[j, 8 * p:8 * p + 8, :, :].rearrange("b r d -> (b r) d"), 120)
                for g in range(4):
                    plan_w(("pool", p, j, g), lambda j=j, g=g: wview(w_pool[j, g, :, :]), (8, 1024))
            else:
                plan_in(("stc", p, j),
                        lambda j=j, p=p: stc[j, 8 * p:8 * p + 8, :, :].rearrange("b r d -> (b r) d"), 16)
                for cb in range(8):
                    for s3 in (2, 1, 0):
                        for hf in range(2):
                            r0 = (((j * 3 + s3) * 8 + cb) * 2 + hf) * 128
                            plan_w(("cin", p, j, cb, s3, hf),
                                   lambda r0=r0: w_cin[r0:r0 + 128, :].rearrange("p (a b) -> p a b", a=4), (4, 2048))
                    for dq in range(4):
                        r0 = ((j * 8 + cb) * 4 + dq) * 128
                        plan_w(("cout", p, j, cb, dq),
                               lambda r0=r0: w_cout[r0:r0 + 128, :].rearrange("p (a b) -> p a b", a=2), (2, 2048))
            for blk in range(32):
                for hf in range(2):
                    r0 = ((i * 32 + blk) * 2 + hf) * 128
                    plan_w(("up", p, i, blk, hf),
                           lambda r0=r0: w_up[r0:r0 + 128, :].rearrange("p (a b) -> p a b", a=4), (4, 2048))
                for dq in range(4):
                    r0 = ((i * 32 + blk) * 4 + dq) * 128
                    plan_w(("down", p, i, blk, dq),
                           lambda r0=r0: w_down[r0:r0 + 128, :].rearrange("p (a b) -> p a b", a=2), (2, 2048))
        for bi in range(len(out_blocks(p))):
            plan_out(("yout", p, bi))

    def mm_group(s, lhs_fn, rhs_fn, nk, rkeys_fn, ks=None):
        for k in (range(nk) if ks is None else ks):
            lhsT = lhs_fn(k)
            rk = rkeys_fn(k)
            for (c0, c1) in P.SPL:
                rhs = rhs_fn(k, c0, c1)
                S.add("pe",
                      lambda e, lhsT=lhsT, rhs=rhs, c0=c0, c1=c1, k=k: e.matmul(
                          ps(s)[:, c0:c1], lhsT=lhsT, rhs=rhs, start=(k == 0), stop=(k == nk - 1)),
                      r=rk, w=[("ps", s)])

    def norm_stats(need_r2):
        s = next_ps()
        for c in range(NCH):
            b = c % 2
            S.add("act", lambda e, c=c, b=b: e.activation(out=sq_t[:, b, 0:P.T], in_=x_t[:, c, 0:P.T], func=AF.Square),
                  r=[("x", c)], w=[("sq", b)])
            for (c0, c1) in P.SPL:
                S.add("pe", lambda e, c=c, b=b, c0=c0, c1=c1: e.matmul(
                    ps(s)[:, c0:c1], lhsT=ones_t, rhs=sq_t[:, b, c0:c1], start=(c == 0), stop=(c == NCH - 1)),
                    r=[("sq", b), ("c", "ones")], w=[("ps", s)])
        S.add("act", lambda e: e.activation(out=rstd_f[:, 0:P.T], in_=ps(s)[:, 0:P.T], func=AF.Sqrt, scale=1.0 / D,
                                            bias=eps_t[:, 0:1]),
              r=[("c", "eps")], w=[("ps", s), ("rstd",)])
        S.add("dve", lambda e: e.reciprocal(out=rstd_f[:, 0:P.T], in_=rstd_f[:, 0:P.T]), w=[("rstd",)])
        if need_r2:
            S.add("act", lambda e: e.activation(out=rstd2_f[:, 0:P.T], in_=rstd_f[:, 0:P.T], func=AF.Square),
                  r=[("rstd",)], w=[("rstd2",)])

    def make_u(g_t, gi, gname):
        for c in range(NCH):
            S.add("dve", lambda e, c=c: e.tensor_scalar(out=u_t[:, c, 0:P.T], in0=x_t[:, c, 0:P.T],
                                                        scalar1=g_t[:, gi * NCH + c:gi * NCH + c + 1],
                                                        scalar2=None, op0=ALU.mult),
                  r=[("x", c), ("c", gname)], w=[("u", c)])

    def half_tiles(key_fn):
        for hf in range(2):
            t = R.next(key_fn(hf))
            v = ring_bf(t, 32 * 256).rearrange("p (k m) -> p k m", k=32)
            for j2 in range(2):
                yield hf * 2 + j2, j2, v, t
            R.free(t)

    def down_phase(key_fn, hb):
        for dh in range(4):
            Dt = R.next(key_fn(dh))
            dv = ring_bf(Dt, 4 * 1024).rearrange("p (j m) -> p j m", j=4)
            def grp(s, mm, ks):
                mm_group(s,
                         lambda k: dv[:, k, mm * 128:(mm + 1) * 128],
                         lambda k, c0, c1: h_t[:, hb, k, c0:c1],
                         4,
                         lambda k: Dt.keys + (("h", hb, k),), ks=ks)

            def evac(s, m):
                S.add("dve", lambda e: e.tensor_tensor(out=x_t[:, m, 0:P.T], in0=ps(s)[:, 0:P.T],
                                                       in1=x_t[:, m, 0:P.T], op=ALU.add),
                      w=[("ps", s), ("x", m)])

            mm = 0
            while mm < 8:
                if dh == 0 and mm == 0:
                    s0, s1 = next_ps(), next_ps()
                    grp(s0, 0, (0, 1, 2))
                    grp(s1, 1, (0, 1, 2))
                    grp(s0, 0, (3,))
                    grp(s1, 1, (3,))
                    evac(s0, 0)
                    evac(s1, 1)
                    mm = 2
                else:
                    s = next_ps()
                    grp(s, mm, None)
                    evac(s, dh * 8 + mm)
                    mm += 1
            R.free(Dt)

    blk_ctr = [0]

    def ffn(p, i):
        norm_stats(False)
        make_u(gmlp_t, i, "gmlp")
        for blk in range(32):
            hb = 0
            for jj, j2, uv, U in half_tiles(lambda hf: ("up", p, i, blk, hf)):
                s = next_ps()
                mm_group(s,
                         lambda k: uv[:, k, j2 * 128:(j2 + 1) * 128],
                         lambda k, c0, c1: u_t[:, k, c0:c1],
                         32,
                         lambda k: U.keys + (("u", k),))
                sl = scr_ctr[0] % 5
                scr_ctr[0] += 1
                S.add("dve", lambda e, s=s, sl=sl: e.scalar_tensor_tensor(
                    out=scr_t[:, sl, PAD:PAD + P.T], in0=ps(s)[:, 0:P.T], scalar=0.0, in1=rstd_f[:, 0:P.T],
                    op0=ALU.max, op1=ALU.mult),
                    r=[("rstd",)], w=[("ps", s), ("scr", sl)])
                S.add("act", lambda e, sl=sl, hb=hb, jj=jj: e.activation(
                    out=h_t[:, hb, jj, 0:P.T], in_=scr_t[:, sl, PAD:PAD + P.T], func=AF.Square),
                    r=[("scr", sl)], w=[("h", hb, jj)])
            down_phase(lambda dh: ("down", p, i, blk, dh), hb)

    def conv_layer(p, i):
        j = i // 2
        norm_stats(True)
        make_u(gmix_t, i, "gmix")
        st = R.next(("stc", p, j))
        stv = ring_f32(st, D)
        for cg in range(8):
            s = next_ps()
            for q in range(4):
                c = cg * 4 + q
                o_ = ps(s)[:, q * 16:(q + 1) * 16]
                i_ = stv[0:16, c * 128:(c + 1) * 128]
                S.add("pe", lambda e, o_=o_, i_=i_: e.transpose(out=o_, in_=i_, identity=idn_t[0:16, 0:16]),
                      r=st.keys + (("c", "idn"),), w=[("ps", s)])
            o_ = vst_t[:, cg * 4:cg * 4 + 4, :]
            i_ = ps(s)[:, 0:64].rearrange("p (q r) -> p q r", q=4)
            S.add("act", lambda e, o_=o_, i_=i_: e.copy(out=o_, in_=i_), w=[("ps", s)] + TLK)
        R.free(st)
        cwb = j * 3 * NCH

        def wk(kk, c):
            return cw_t[:, cwb + kk * NCH + c:cwb + kk * NCH + c + 1]

        for cb in range(8):
            hb = 0
            for jj, j2, hv, Ht in half_tiles(lambda hf: ("cin", p, j, cb, 2, hf)):
                s = next_ps()
                mm_group(s, lambda k: hv[:, k, j2 * 128:(j2 + 1) * 128],
                         lambda k, c0, c1: u_t[:, k, c0:c1], 32, lambda k: Ht.keys + (("u", k),))
                S.add("dve", lambda e, s=s, jj=jj: e.tensor_tensor(
                    out=scr_t[:, jj, PAD:PAD + P.T], in0=ps(s)[:, 0:P.T], in1=rstd2_f[:, 0:P.T], op=ALU.mult),
                    r=[("rstd2",)], w=[("ps", s), ("scr", jj)])
            cvs = (4, 0, 1, 2)
            for jj, j2, cv_, Ct in half_tiles(lambda hf: ("cin", p, j, cb, 1, hf)):
                c = cb * 4 + jj
                s = next_ps()
                mm_group(s, lambda k: cv_[:, k, j2 * 128:(j2 + 1) * 128],
                         lambda k, c0, c1: u_t[:, k, c0:c1], 32, lambda k: Ct.keys + (("u", k),))
                S.add("dve", lambda e, s=s, jj=jj: e.tensor_tensor(
                    out=scr_t[:, jj, PAD:PAD + P.T], in0=ps(s)[:, 0:P.T], in1=scr_t[:, jj, PAD:PAD + P.T], op=ALU.mult),
                    w=[("ps", s), ("scr", jj)])
                V = scr_t[:, jj, :]
                cs = cvs[jj]
                CV = scr_t[:, cs, :]
                if p == 0:
                    S.add("act", lambda e, V=V, c=c: e.copy(out=csv_t[:, j, c, :], in_=V[:, PAD + P.PC - 2:PAD + P.PC]),
                          r=[("scr", jj)], w=[("csv", j, c)])
                else:
                    S.add("act", lambda e, V=V, c=c: e.copy(out=V[:, PAD - 2:PAD], in_=csv_t[:, j, c, :]),
                          r=[("csv", j, c)], w=[("scr", jj)])
                S.add("act", lambda e, c=c: e.copy(out=vtl_t[:, :, 0:2],
                                                   in_=vst_t[:, c, :].rearrange("p (b r) -> p b r", b=8)),
                      r=TLK, w=[("vtl",)])
                S.add("act", lambda e, V=V: e.copy(out=vtl_t[:, :, 2:6],
                                                   in_=V[:, PAD + P.PC:PAD + P.T].rearrange("p (b r) -> p b r", b=8)),
                      r=[("scr", jj)], w=[("vtl",)])
                vb_ = c % 2
                S.add("act", lambda e, V=V, vb_=vb_: e.copy(out=vsv_t[:, vb_, 0:2], in_=V[:, PAD + P.PC - 2:PAD + P.PC]),
                      r=[("scr", jj)], w=[("vsv", vb_)])
                S.add("act", lambda e, V=V, vb_=vb_: e.copy(
                    out=vsv_t[:, vb_, 2:18].rearrange("p (b r) -> p b r", b=8),
                    in_=V[:, PAD + P.PC:PAD + P.T].rearrange("p (b r) -> p b r", b=8)[:, :, 2:4]),
                    r=[("scr", jj)], w=[("vsv", vb_)])
                s3 = next_ps()
                S.add("pe", lambda e, s3=s3, vb_=vb_: e.transpose(out=ps(s3)[0:18, 0:128], in_=vsv_t[:, vb_, :],
                                                                  identity=idn_t),
                      r=[("vsv", vb_), ("c", "idn")], w=[("ps", s3)])
                ob = (c // 2) % 2
                S.add("act", lambda e, s3=s3, ob=ob, c=c: e.copy(out=ost_t[0:18, ob, (c % 2) * 128:(c % 2 + 1) * 128],
                                                                 in_=ps(s3)[0:18, 0:128]),
                      w=[("ps", s3), ("ost", ob)])
                if c % 2 == 1:
                    col = (c - 1) * 128
                    if p == NPASS - 1:
                        S.add("sp", lambda e, ob=ob, col=col: e.dma_start(out=conv_p[j, :, col:col + 256],
                                                                           in_=ost_t[0:2, ob, :]),
                              r=[("ost", ob)], dma=True)
                    S.add("sp", lambda e, ob=ob, col=col: e.dma_start(
                        out=conv_s[j, 16 * p:16 * p + 16, col:col + 256], in_=ost_t[2:18, ob, :]),
                        r=[("ost", ob)], dma=True)
                S.add("act", lambda e, V=V, CV=CV, c=c: e.activation(
                    out=CV[:, PAD:PAD + P.PC], in_=V[:, PAD - 2:PAD + P.PC - 2], func=AF.Copy, scale=wk(0, c)),
                    r=[("scr", jj), ("c", "cw")], w=[("scr", cs)])
                S.add("dve", lambda e, V=V, CV=CV, c=c: e.scalar_tensor_tensor(
                    out=CV[:, PAD:PAD + P.PC], in0=V[:, PAD - 1:PAD + P.PC - 1], scalar=wk(1, c), in1=CV[:, PAD:PAD + P.PC],
                    op0=ALU.mult, op1=ALU.add),
                    r=[("scr", jj), ("c", "cw")], w=[("scr", cs)])
                S.add("dve", lambda e, V=V, CV=CV, c=c: e.scalar_tensor_tensor(
                    out=CV[:, PAD:PAD + P.PC], in0=V[:, PAD:PAD + P.PC], scalar=wk(2, c), in1=CV[:, PAD:PAD + P.PC],
                    op0=ALU.mult, op1=ALU.add),
                    r=[("scr", jj), ("c", "cw")], w=[("scr", cs)])
                CVs = CV[:, PAD + P.PC:PAD + P.T].rearrange("p (b r) -> p b r", b=8)
                S.add("dve", lambda e, CVs=CVs, c=c: e.tensor_scalar(
                    out=CVs, in0=vtl_t[:, :, 0:4], scalar1=wk(0, c), scalar2=None, op0=ALU.mult),
                    r=[("vtl",), ("c", "cw")], w=[("scr", cs)])
                S.add("dve", lambda e, CVs=CVs, c=c: e.scalar_tensor_tensor(
                    out=CVs, in0=vtl_t[:, :, 1:5], scalar=wk(1, c), in1=CVs, op0=ALU.mult, op1=ALU.add),
                    r=[("vtl",), ("c", "cw")], w=[("scr", cs)])
                S.add("dve", lambda e, CVs=CVs, c=c: e.scalar_tensor_tensor(
                    out=CVs, in0=vtl_t[:, :, 2:6], scalar=wk(2, c), in1=CVs, op0=ALU.mult, op1=ALU.add),
                    r=[("vtl",), ("c", "cw")], w=[("scr", cs)])
                S.add("dve", lambda e, CV=CV: e.tensor_tensor(out=CV[:, PAD:PAD + P.T], in0=CV[:, PAD:PAD + P.T],
                                                              in1=rstd_f[:, 0:P.T], op=ALU.mult),
                      r=[("rstd",)], w=[("scr", cs)])
            for jj, j2, bv, Bt in half_tiles(lambda hf: ("cin", p, j, cb, 0, hf)):
                s = next_ps()
                cs = cvs[jj]
                mm_group(s, lambda k: bv[:, k, j2 * 128:(j2 + 1) * 128],
                         lambda k, c0, c1: u_t[:, k, c0:c1], 32, lambda k: Bt.keys + (("u", k),))
                S.add("dve", lambda e, s=s, cs=cs, hb=hb, jj=jj: e.tensor_tensor(
                    out=h_t[:, hb, jj, 0:P.T], in0=ps(s)[:, 0:P.T], in1=scr_t[:, cs, PAD:PAD + P.T], op=ALU.mult),
                    r=[("scr", cs)], w=[("ps", s), ("h", hb, jj)])
            down_phase(lambda dh: ("cout", p, j, cb, dh), hb)

    def pool_layer(p, i):
        j = i // 2
        S.add("sp", lambda e: e.dma_start(out=pos1_f[:, 0:P.T], in_=pos[p:p + 1, 0:P.T].partition_broadcast(128)),
              w=[("rstd2",)], dma=True)
        S.add("dve", lambda e: e.tensor_scalar(out=pos1_f[:, 0:P.T], in0=pos1_f[:, 0:P.T], scalar1=1.0, scalar2=1.0,
                                               op0=ALU.add, op1=ALU.max), w=[("rstd2",)])
        norm_stats(False)
        st = R.next(("stp", p, j))
        stv = ring_f32(st, D)
        if p == 0:
            S.add("sp", lambda e: e.dma_start(out=pool_so[j, :, :, :], in_=stp[j, :, 4:15, :]), dma=True)
        A, B_, C_, Dn = 0, 1, 2, 3
        for g in range(4):
            w = WINS[g]
            S.add("dve", lambda e, w=w: e.tensor_scalar(out=scr_t[:, Dn, PAD:PAD + P.PC], in0=pos1_f[:, 0:P.PC],
                                                        scalar1=float(w), scalar2=None, op0=ALU.min),
                  r=[("rstd2",)], w=[("scr", Dn)])
            S.add("dve", lambda e: e.reciprocal(out=scr_t[:, Dn, PAD:PAD + P.PC], in_=scr_t[:, Dn, PAD:PAD + P.PC]),
                  w=[("scr", Dn)])
            for cc in range(8):
                c = g * 8 + cc
                gcol = i * NCH + c
                S.add("dve", lambda e, c=c, gcol=gcol: e.scalar_tensor_tensor(
                    out=scr_t[:, A, PAD:PAD + P.T], in0=x_t[:, c, 0:P.T], scalar=gmix_t[:, gcol:gcol + 1], in1=rstd_f[:, 0:P.T],
                    op0=ALU.mult, op1=ALU.mult),
                    r=[("x", c), ("rstd",), ("c", "gmix")], w=[("scr", A)])
                s2 = next_ps()
                S.add("pe", lambda e, s2=s2, c=c: e.transpose(out=ps(s2)[:, 0:120],
                                                               in_=stv[0:120, c * 128:(c + 1) * 128],
                                                               identity=idn_t[0:120, 0:120]),
                      r=st.keys + (("c", "idn"),), w=[("ps", s2)])
                S.add("act", lambda e, s2=s2: e.copy(out=tl_t[:, 0, :, 0:15],
                                                     in_=ps(s2)[:, 0:120].rearrange("p (b r) -> p b r", b=8)),
                      w=[("ps", s2), ("tl", 0)])
                S.add("act", lambda e: e.copy(out=tl_t[:, 0, :, 15:19],
                                              in_=scr_t[:, A, PAD + P.PC:PAD + P.T].rearrange("p (b r) -> p b r", b=8)),
                      r=[("scr", A)], w=[("tl", 0)])
                s3 = next_ps()
                S.add("pe", lambda e, s3=s3: e.transpose(out=ps(s3)[0:47, 0:128], in_=scr_t[:, A, PAD + P.PC - 15:PAD + P.T],
                                                         identity=idn_t),
                      r=[("scr", A), ("c", "idn")], w=[("ps", s3)])
                ob = (c // 2) % 2
                S.add("act", lambda e, s3=s3, ob=ob, c=c: e.copy(out=ost_t[0:47, ob, (c % 2) * 128:(c % 2 + 1) * 128],
                                                                 in_=ps(s3)[0:47, 0:128]),
                      w=[("ps", s3), ("ost", ob)])
                if c % 2 == 1:
                    col = (c - 1) * 128
                    if p == NPASS - 1:
                        S.add("sp", lambda e, ob=ob, col=col: e.dma_start(out=pool_p[j, :, col:col + 256],
                                                                           in_=ost_t[0:15, ob, :]),
                              r=[("ost", ob)], dma=True)
                    S.add("sp", lambda e, ob=ob, col=col: e.dma_start(
                        out=pool_sn[j, NSP * p:NSP * (p + 1), col:col + 256], in_=ost_t[15:47, ob, :]),
                        r=[("ost", ob)], dma=True)
                if p == 0:
                    S.add("act", lambda e, c=c: e.copy(out=psv_t[:, j, c, :],
                                                       in_=scr_t[:, A, PAD + P.PC - 15:PAD + P.PC]),
                          r=[("scr", A)], w=[("psv", j, c)])
                else:
                    S.add("act", lambda e, c=c: e.copy(out=scr_t[:, A, PAD - 15:PAD], in_=psv_t[:, j, c, :]),
                          r=[("psv", j, c)], w=[("scr", A)])
                cur = A
                nxt_slots = [B_, C_]
                for si in range(g + 1):
                    sft = 1 << si
                    lo = PAD - 15 + 2 * sft - 1
                    o_ = nxt_slots[si % 2]
                    S.add("dve", lambda e, cur=cur, o_=o_, sft=sft, lo=lo: e.tensor_tensor(
                        out=scr_t[:, o_, lo:PAD + P.PC], in0=scr_t[:, cur, lo:PAD + P.PC],
                        in1=scr_t[:, cur, lo - sft:PAD + P.PC - sft], op=ALU.add),
                        r=[("scr", cur)], w=[("scr", o_)])
                    cur = o_
                o_ = B_ if cur == C_ else C_
                S.add("dve", lambda e, cur=cur, o_=o_: e.tensor_tensor(
                    out=scr_t[:, o_, PAD:PAD + P.PC], in0=scr_t[:, cur, PAD:PAD + P.PC], in1=scr_t[:, Dn, PAD:PAD + P.PC],
                    op=ALU.mult),
                    r=[("scr", cur), ("scr", Dn)], w=[("scr", o_)])
                S.add("dve", lambda e, o_=o_, c=c: e.tensor_tensor(
                    out=u_t[:, c, 0:P.PC], in0=scr_t[:, o_, PAD:PAD + P.PC], in1=scr_t[:, A, PAD:PAD + P.PC],
                    op=ALU.subtract),
                    r=[("scr", o_), ("scr", A)], w=[("u", c)])
                tc = 0
                for si in range(g + 1):
                    sft = 1 << si
                    lo = 2 * sft - 1
                    to = 1 if tc != 1 else 2
                    S.add("dve", lambda e, tc=tc, to=to, sft=sft, lo=lo: e.tensor_tensor(
                        out=tl_t[:, to, :, lo:19], in0=tl_t[:, tc, :, lo:19], in1=tl_t[:, tc, :, lo - sft:19 - sft],
                        op=ALU.add),
                        r=[("tl", tc)], w=[("tl", to)])
                    tc = to
                S.add("dve", lambda e, tc=tc, c=c, w=w: e.scalar_tensor_tensor(
                    out=u_t[:, c, P.PC:P.T].rearrange("p (b r) -> p b r", b=8), in0=tl_t[:, tc, :, 15:19],
                    scalar=1.0 / w, in1=tl_t[:, 0, :, 15:19], op0=ALU.mult, op1=ALU.subtract),
                    r=[("tl", tc), ("tl", 0)], w=[("u", c)])
            if g == 3:
                R.free(st)
            Wg = R.next(("pool", p, j, g))
            wv = ring_bf(Wg, 8 * 1024).rearrange("p (k m) -> p k m", k=8)
            for mm in range(8):
                m = g * 8 + mm
                s = next_ps()
                mm_group(s, lambda k, mm=mm: wv[:, k, mm * 128:(mm + 1) * 128],
                         lambda k, c0, c1: u_t[:, g * 8 + k, c0:c1], 8,
                         lambda k: Wg.keys + (("u", g * 8 + k),))
                S.add("dve", lambda e, s=s, m=m: e.scalar_tensor_tensor(
                    out=x_t[:, m, 0:P.T], in0=ps(s)[:, 0:P.T], scalar=psc_t[:, j * NCH + m:j * NCH + m + 1],
                    in1=x_t[:, m, 0:P.T], op0=ALU.mult, op1=ALU.add),
                    r=[("c", "psc")], w=[("ps", s), ("x", m)])
            R.free(Wg)

    def load_x(p):
        for bi, (src_fn, nr, c0) in enumerate(in_blocks(p)):
            t = R.next(("xin", p, bi))
            tv = ring_f32(t, D)
            for cg in range(8):
                s = next_ps()
                for q in range(4):
                    c = cg * 4 + q
                    o_ = ps(s)[:, q * 128:q * 128 + nr]
                    i_ = tv[0:nr, c * 128:(c + 1) * 128]
                    id_ = idn_t[0:nr, 0:nr]
                    S.add("pe", lambda e, o_=o_, i_=i_, id_=id_: e.transpose(out=o_, in_=i_, identity=id_),
                          r=t.keys + (("c", "idn"),), w=[("ps", s)])
                eng = "act" if cg % 2 else "dve"
                src = ps(s)[:, 0:512].rearrange("p (q n) -> p q n", q=4)[:, :, 0:nr]
                dst = x_t[:, cg * 4:cg * 4 + 4, c0:c0 + nr]
                if eng == "act":
                    S.add("act", lambda e, src=src, dst=dst: e.copy(out=dst, in_=src),
                          w=[("ps", s)] + [("x", cg * 4 + q) for q in range(4)])
                else:
                    S.add("dve", lambda e, src=src, dst=dst: e.tensor_copy(out=dst, in_=src),
                          w=[("ps", s)] + [("x", cg * 4 + q) for q in range(4)])
            R.free(t)

    def final_out(p):
        norm_stats(False)
        for c in range(NCH):
            S.add("dve", lambda e, c=c: e.scalar_tensor_tensor(
                out=x_t[:, c, 0:P.T], in0=x_t[:, c, 0:P.T], scalar=gfin_t[:, c:c + 1], in1=rstd_f[:, 0:P.T],
                op0=ALU.mult, op1=ALU.mult),
                r=[("rstd",), ("c", "gfin")], w=[("x", c)])
        for bi, (dst_ap, n, c0) in enumerate(out_blocks(p)):
            t = R.next(("yout", p, bi))
            tv = ring_f32(t, D)
            for cg in range(8):
                s = next_ps()
                for q in range(4):
                    c = cg * 4 + q
                    o_ = ps(s)[0:n, q * 128:(q + 1) * 128]
                    i_ = x_t[:, c, c0:c0 + n]
                    S.add("pe", lambda e, o_=o_, i_=i_: e.transpose(out=o_, in_=i_, identity=idn_t),
                          r=[("x", c), ("c", "idn")], w=[("ps", s)])
                eng = "act" if cg % 2 else "dve"
                src = ps(s)[0:n, 0:512]
                dst = tv[0:n, cg * 512:(cg + 1) * 512]
                if eng == "act":
                    S.add("act", lambda e, src=src, dst=dst: e.copy(out=dst, in_=src), w=[("ps", s)] + list(t.keys))
                else:
                    S.add("dve", lambda e, src=src, dst=dst: e.tensor_copy(out=dst, in_=src),
                          w=[("ps", s)] + list(t.keys))
            i_ = tv[0:n, :]
            S.add("sp", lambda e, dst_ap=dst_ap, i_=i_: e.dma_start(out=dst_ap, in_=i_), r=t.keys, dma=True)
            R.free(t)

    R.pump()
    for p in range(NPASS):
        P.T, P.PC, P.SPL = TS[p], PCS[p], SPLS[p]
        load_x(p)
        for i in range(DEPTH):
            if i % 2 == 0:
                pool_layer(p, i)
            else:
                conv_layer(p, i)
            ffn(p, i)
        final_out(p)
    assert R.cons == len(R.plan)

    S.finalize()

    sem_ctxs = {}
    sems = {}
    for name in ("pe", "act", "dve"):
        c = nc.semaphore("s_" + name)
        sem_ctxs[name] = c
        sems[name] = c.__enter__()
    for q, n in (("pool", N_WSEM), ("sp", N_IOSEM)):
        for i in range(n):
            c = nc.semaphore("s_%s%d" % (q, i))
            sem_ctxs[(q, i)] = c
            sems[(q, i)] = c.__enter__()
    with nc.Block() as block:
        S.emit(nc, block, sems)
    for c in reversed(list(sem_ctxs.values())):
        c.__exit__(None, None, None)
    ctx_ps.__exit__(None, None, None)
    ctx_arena.__exit__(None, None, None)
    return nc


_NC_CACHE = {}


def _layout_vec(v):
    v = np.asarray(v, dtype=np.float32)
    if v.ndim == 1:
        v = v[None, :]
    n = v.shape[0]
    return np.ascontiguousarray(v.reshape(n, NCH, 128).transpose(2, 0, 1).reshape(128, n * NCH))


def kernel(x_prompt, x_sample, state_pool, state_conv, norm_mix, norm_mlp, norm_final,
           w_pool, pool_scale, w_conv_in, conv_w, w_conv_out, w_up, w_down):
    n = 8
    x_prompt = np.asarray(x_prompt, dtype=np.float32)
    x_sample = np.asarray(x_sample, dtype=np.float32)
    state_pool = np.asarray(state_pool, dtype=np.float32)
    state_conv = np.asarray(state_conv, dtype=np.float32)
    if "nc" not in _NC_CACHE:
        _NC_CACHE["nc"] = build_nc()
    nc = _NC_CACHE["nc"]

    gmix = _layout_vec(norm_mix)
    gmlp = _layout_vec(norm_mlp)
    gfin = _layout_vec(norm_final)
    psc = _layout_vec(pool_scale)
    cw = _layout_vec(np.asarray(conv_w, dtype=np.float32).reshape(6, D))
    ident = np.eye(128, dtype=np.float32)
    def up_like(w, nb):
        L = w.shape[0]
        v = np.asarray(w, dtype=np.float32).reshape(L, 32, 128, nb, 2, 256)
        return np.ascontiguousarray(v.transpose(0, 3, 4, 2, 1, 5)).reshape(L * nb * 2 * 128, 32 * 256)

    def down_like(w, nb):
        L = w.shape[0]
        v = np.asarray(w, dtype=np.float32).reshape(L, nb, 4, 128, 4, 1024)
        return np.ascontiguousarray(v.transpose(0, 1, 4, 3, 2, 5)).reshape(L * nb * 4 * 128, 4 * 1024)

    wci = np.asarray(w_conv_in, dtype=np.float32).reshape(2, D, 3, D).transpose(0, 2, 1, 3).reshape(6, D, D)
    weights = {
        "w_pool": np.asarray(w_pool, dtype=np.float32),
        "w_conv_in": up_like(wci, 8),
        "w_conv_out": down_like(w_conv_out, 8),
        "w_up": up_like(w_up, 32),
        "w_down": down_like(w_down, 32),
    }
    del wci
    in_maps = []
    for c in range(n):
        seq, half = c // 2, c % 2
        xp = np.zeros((HALO + 1024, D), np.float32)
        lo = half * 1024 - HALO
        if lo < 0:
            xp[HALO:] = x_prompt[seq, 0:1024]
        else:
            xp[:] = x_prompt[seq, lo:lo + HALO + 1024]
        pos = np.zeros((NPASS, TMAX), np.float32)
        pos[0, 0:PCS[0]] = half * 1024 - HALO + np.arange(PCS[0])
        pos[1, 0:PCS[1]] = half * 1024 + NTOK[0] + np.arange(PCS[1])
        for p in range(NPASS):
            pos[p, PCS[p]:TS[p]] = 16384 + np.tile(np.arange(4), 8)
        m = {
            "xp": xp,
            "xs": np.ascontiguousarray(x_sample[16 * c:16 * c + 16].reshape(64, D)),
            "stp": np.ascontiguousarray(state_pool[:, 16 * c:16 * c + 16]),
            "stc": np.ascontiguousarray(state_conv[:, 16 * c:16 * c + 16]),
            "pos": pos,
            "gmix": gmix, "gmlp": gmlp, "gfin": gfin, "pscale": psc, "convw": cw, "ident": ident,
        }
        m.update(weights)
        in_maps.append(m)

    res = run_bass_kernel_spmd(nc, in_maps, core_ids=list(range(n)))
    outs = res.results

    y_prompt = np.empty((4, 2048, D), np.float32)
    y_sample = np.empty((128, 4, D), np.float32)
    new_pool_prompt = np.empty((2, 4, 15, D), np.float32)
    new_pool_sample = np.empty((2, 128, 15, D), np.float32)
    new_conv_prompt = np.empty((2, 4, 2, D), np.float32)
    new_conv_sample = np.empty((2, 128, 2, D), np.float32)
    for c in range(n):
        seq, half = c // 2, c % 2
        o = outs[c]
        y_prompt[seq, half * 1024:(half + 1) * 1024] = o["yp"]
        y_sample[16 * c:16 * c + 16] = o["ys"].reshape(16, 4, D)
        new_pool_sample[:, 16 * c:16 * c + 16, 0:11] = o["pool_so"]
        new_pool_sample[:, 16 * c:16 * c + 16, 11:15] = o["pool_sn"].reshape(2, 16, 4, D)
        new_conv_sample[:, 16 * c:16 * c + 16] = o["conv_s"].reshape(2, 16, 2, D)
        if half == 1:
            new_pool_prompt[:, seq] = o["pool_p"]
            new_conv_prompt[:, seq] = o["conv_p"]
    return (y_prompt, y_sample, new_pool_prompt, new_pool_sample, new_conv_prompt, new_conv_sample)
```

```python
import numpy as np
import concourse.bass as bass
import concourse.mybir as mybir
from concourse.bass_utils import run_bass_kernel_spmd

F32 = mybir.dt.float32
BF16 = mybir.dt.bfloat16
AF = mybir.ActivationFunctionType
ALU = mybir.AluOpType

D = 4096
NCH = 32
DFF = 16384
DEPTH = 4
HALO = 34
NSP = 32
NPASS = 2
NTOK = (446, 578)
PCS = (HALO + 446, 578)
TS = (PCS[0] + NSP, PCS[1] + NSP)
SPLS = (((0, 512),), ((0, 512), (512, 610)))
TMAX = max(TS)
PAD = 16
SW = PAD + TMAX
WINS = (2, 4, 8, 16)
RING_UNITS = 14
UNIT = 4096
EPS = 1e-6
N_WSEM = 12
N_IOSEM = 8


class Op:
    __slots__ = ("eng", "fn", "deps", "mark", "seq", "dma", "sem_idx", "sem_val", "uid")

    def __init__(self, eng, fn, dma, uid):
        self.eng = eng
        self.fn = fn
        self.dma = dma
        self.deps = []
        self.mark = False
        self.seq = 0
        self.sem_idx = 0
        self.sem_val = 0
        self.uid = uid


class _Rec:
    def __getattr__(self, name):
        def call(*a, **kw):
            self.call = (name, a, kw)
            return None
        return call


class Sched:
    ENGS = ("pe", "act", "dve", "pool", "sp")

    def __init__(self):
        self.ops = {e: [] for e in self.ENGS}
        self.last_w = {}
        self.readers = {}
        self.n = 0

    def add(self, eng, fn, r=(), w=(), dma=False):
        self.n += 1
        rec = _Rec()
        fn(rec)
        op = Op(eng, rec.call, dma, self.n)
        deps = op.deps
        seen = set()
        is_pe = (eng == "pe")

        def dep(p):
            if p is None or p.uid in seen:
                return
            if is_pe and p.eng == "pe":
                return
            seen.add(p.uid)
            deps.append(p)
            p.mark = True

        lw = self.last_w
        rdm = self.readers
        for k in r:
            dep(lw.get(k))
        for k in w:
            dep(lw.get(k))
            rd = rdm.get(k)
            if rd:
                for p in rd.values():
                    dep(p)
        for k in w:
            lw[k] = op
            rdm[k] = {}
        rk = ("dma", op.uid) if dma else eng
        for k in r:
            d = rdm.get(k)
            if d is None:
                d = rdm[k] = {}
            d[rk] = op
        self.ops[eng].append(op)
        return op

    def finalize(self):
        for e in ("pe", "act", "dve"):
            c = 0
            for op in self.ops[e]:
                if op.mark:
                    c += 1
                    op.seq = c
        self.dma_final = {}
        for q, nsem in (("pool", N_WSEM), ("sp", N_IOSEM)):
            c = 0
            for op in self.ops[q]:
                assert op.dma
                op.sem_idx = c % nsem
                op.sem_val = 16 * (c // nsem + 1)
                self.dma_final[(q, op.sem_idx)] = op.sem_val
                c += 1

    def emit(self, nc, block, sems):
        def token(p):
            if p.dma:
                return (p.eng, p.sem_idx), p.sem_val
            return p.eng, p.seq

        def run(e, name):
            known = {}
            for op in self.ops[name]:
                need = {}
                for p in op.deps:
                    k, v = token(p)
                    if known.get(k, 0) < v and need.get(k, 0) < v:
                        need[k] = v
                if op.dma and op.sem_val > 16:
                    k = (name, op.sem_idx)
                    v = op.sem_val - 16
                    if known.get(k, 0) < v and need.get(k, 0) < v:
                        need[k] = v
                for k, v in need.items():
                    e.wait_ge(sems[k], v)
                    known[k] = v
                mname, a_, kw_ = op.fn
                ins = getattr(e, mname)(*a_, **kw_)
                if op.dma:
                    ins.then_inc(sems[(name, op.sem_idx)], 16)
                elif op.mark:
                    ins.then_inc(sems[name], 1)
            if name == "sp":
                for (q, i), v in self.dma_final.items():
                    if q == "sp" and known.get((q, i), 0) < v:
                        e.wait_ge(sems[(q, i)], v)

        @block.tensor
        def _(e):
            run(e, "pe")

        @block.scalar
        def _(e):
            run(e, "act")

        @block.vector
        def _(e):
            run(e, "dve")

        @block.gpsimd
        def _(e):
            run(e, "pool")

        @block.sync
        def _(e):
            run(e, "sp")


class Tile:
    __slots__ = ("key", "kind", "size", "start", "issue", "issued", "keys")

    def __init__(self, key, kind, size, issue):
        self.key = key
        self.kind = kind
        self.size = size
        self.issue = issue
        self.issued = False
        self.start = -1
        self.keys = ()


class Ring:
    def __init__(self, n):
        self.n = n
        self.occ = [False] * n
        self.head = 0
        self.plan = []
        self.nxt = 0
        self.cons = 0

    def pump(self):
        while self.nxt < len(self.plan):
            t = self.plan[self.nxt]
            s = -1
            for d in range(self.n):
                c = (self.head + d) % self.n
                if c + t.size <= self.n and not any(self.occ[c:c + t.size]):
                    s = c
                    break
            if s < 0:
                return
            for u in range(s, s + t.size):
                self.occ[u] = True
            self.head = (s + t.size) % self.n
            t.start = s
            t.keys = tuple(("r", u) for u in range(s, s + t.size))
            t.issued = True
            self.nxt += 1
            if t.issue is not None:
                t.issue(t)

    def next(self, key):
        t = self.plan[self.cons]
        assert t.key == key, (t.key, key)
        if not t.issued:
            self.pump()
        assert t.issued, ("ring deadlock", key)
        self.cons += 1
        return t

    def free(self, t):
        for u in range(t.start, t.start + t.size):
            self.occ[u] = False
        self.pump()


def build_nc():
    nc = bass.Bass("TRN2", target_bir_lowering=False)

    def din(name, shape):
        return nc.dram_tensor(name, list(shape), F32, kind="ExternalInput").ap()

    def dout(name, shape):
        return nc.dram_tensor(name, list(shape), F32, kind="ExternalOutput").ap()

    xp = din("xp", (HALO + 1024, D))
    xs = din("xs", (64, D))
    stp = din("stp", (2, 16, 15, D))
    stc = din("stc", (2, 16, 2, D))
    pos = din("pos", (NPASS, TMAX))
    gmix_d = din("gmix", (128, DEPTH * NCH))
    gmlp_d = din("gmlp", (128, DEPTH * NCH))
    gfin_d = din("gfin", (128, NCH))
    psc_d = din("pscale", (128, 2 * NCH))
    cw_d = din("convw", (128, 2 * 3 * NCH))
    idn_d = din("ident", (128, 128))
    w_pool = din("w_pool", (2, 4, 1024, 1024))
    w_cin = din("w_conv_in", (2 * 3 * 8 * 2 * 128, 32 * 256))
    w_cout = din("w_conv_out", (2 * 8 * 4 * 128, 4 * 1024))
    w_up = din("w_up", (DEPTH * 32 * 2 * 128, 32 * 256))
    w_down = din("w_down", (DEPTH * 32 * 4 * 128, 4 * 1024))

    yp = dout("yp", (1024, D))
    ys = dout("ys", (64, D))
    pool_p = dout("pool_p", (2, 15, D))
    pool_sn = dout("pool_sn", (2, NPASS * NSP, D))
    pool_so = dout("pool_so", (2, 16, 11, D))
    conv_p = dout("conv_p", (2, 2, D))
    conv_s = dout("conv_s", (2, 32, D))

    S = Sched()
    R = Ring(RING_UNITS)

    off = [0]

    def carve(nbytes):
        o = off[0]
        off[0] = o + ((nbytes + 31) // 32) * 32
        return o

    o_x = carve(NCH * TMAX * 4)
    o_u = carve(NCH * TMAX * 2)
    o_h = carve(4 * TMAX * 2)
    o_scr = carve(5 * SW * 4)
    o_rstd = carve(TMAX * 4)
    o_rstd2 = carve(TMAX * 4)
    o_sq = carve(2 * TMAX * 2)
    o_psv = carve(2 * NCH * 15 * 4)
    o_csv = carve(2 * NCH * 2 * 4)
    o_ost = carve(2 * 256 * 4)
    o_tl = carve(NCH * 16 * 4)
    o_vtl = carve(8 * 6 * 4)
    o_vsv = carve(2 * 18 * 4)
    o_ones = carve(128 * 2)
    o_idn = carve(128 * 4)
    o_gmix = carve(DEPTH * NCH * 4)
    o_gmlp = carve(DEPTH * NCH * 4)
    o_gfin = carve(NCH * 4)
    o_psc = carve(2 * NCH * 4)
    o_cw = carve(6 * NCH * 4)
    o_eps = carve(32)
    o_ring = carve(RING_UNITS * UNIT)
    total = off[0]
    assert total <= 212800, total
    print("[kernel] sbuf bytes/partition:", total, "ring units:", RING_UNITS, "x", UNIT)

    ctx_arena = nc.sbuf_tensor("arena", [128, total // 4], F32)
    ctx_ps = nc.psum_tensor("ps", [128, 4, 1024], F32)
    arena = ctx_arena.__enter__()
    pst = ctx_ps.__enter__()

    def vf(o, n):
        return arena[:, o // 4:o // 4 + n]

    def vb(o, n):
        return arena[:, o // 4:o // 4 + n // 2].bitcast(BF16)

    x_t = vf(o_x, NCH * TMAX).rearrange("p (c t) -> p c t", c=NCH)
    u_t = vb(o_u, NCH * TMAX).rearrange("p (c t) -> p c t", c=NCH)
    h_t = vb(o_h, 4 * TMAX).rearrange("p (b j t) -> p b j t", b=1, j=4)
    scr_t = vf(o_scr, 5 * SW).rearrange("p (s t) -> p s t", s=5)
    rstd_f = vf(o_rstd, TMAX)
    rstd2_f = vf(o_rstd2, TMAX)
    sq_t = vb(o_sq, 2 * TMAX).rearrange("p (s t) -> p s t", s=2)
    psv_t = vf(o_psv, 2 * NCH * 15).rearrange("p (j c r) -> p j c r", j=2, c=NCH)
    csv_t = vf(o_csv, 2 * NCH * 2).rearrange("p (j c r) -> p j c r", j=2, c=NCH)

    class P:
        T = TS[0]
        PC = PCS[0]
        SPL = SPLS[0]
    pos1_f = rstd2_f
    ost_t = vf(o_ost, 512).rearrange("p (s t) -> p s t", s=2)
    tl_t = vf(o_tl, 480).rearrange("p (s b t) -> p s b t", s=3, b=8)
    vtl_t = vf(o_vtl, 48).rearrange("p (b t) -> p b t", b=8)
    vsv_t = vf(o_vsv, 36).rearrange("p (s t) -> p s t", s=2)
    vst_t = vf(o_tl, NCH * 16).rearrange("p (c r) -> p c r", c=NCH)
    TLK = [("tl", 0), ("tl", 1), ("tl", 2)]
    ones_t = vb(o_ones, 128)
    idn_t = vf(o_idn, 128)
    gmix_t = vf(o_gmix, DEPTH * NCH)
    gmlp_t = vf(o_gmlp, DEPTH * NCH)
    gfin_t = vf(o_gfin, NCH)
    psc_t = vf(o_psc, 2 * NCH)
    cw_t = vf(o_cw, 6 * NCH)
    eps_t = vf(o_eps, 1)

    def ring_f32(t, n):
        o = o_ring + t.start * UNIT
        return vf(o, n)

    def ring_bf(t, n):
        o = o_ring + t.start * UNIT
        return vb(o, n)

    def ps(s):
        return pst[:, s, :]

    ps_ctr = [0]

    def next_ps():
        s = ps_ctr[0] % 4
        ps_ctr[0] += 1
        return s

    scr_ctr = [0]

    for name, dst, src in (("idn", idn_t, idn_d), ("gmix", gmix_t, gmix_d), ("gmlp", gmlp_t, gmlp_d),
                           ("gfin", gfin_t, gfin_d), ("psc", psc_t, psc_d), ("cw", cw_t, cw_d)):
        S.add("sp", lambda e, dst=dst, src=src: e.dma_start(out=dst, in_=src[:, :]), w=[("c", name)], dma=True)
    S.add("dve", lambda e: e.memset(ones_t, 1.0), w=[("c", "ones")])
    S.add("dve", lambda e: e.memset(eps_t, EPS), w=[("c", "eps")])
    S.add("dve", lambda e: e.memset(vf(o_scr, 5 * SW), 0.0), w=[("scr", i) for i in range(5)])
    S.add("dve", lambda e: e.memset(vf(o_tl, 512), 0.0), w=[("tl", i) for i in range(3)])

    def issue_w(src_fn, shape):
        a, b = shape

        def issue(t):
            dst = ring_bf(t, a * b).rearrange("p (a b) -> p a b", a=a)
            S.add("pool", lambda e: e.dma_start(out=dst, in_=src_fn()), w=t.keys, dma=True)
        return issue

    def plan_w(key, src_fn, shape):
        a, b = shape
        size = (a * b * 2 + UNIT - 1) // UNIT
        R.plan.append(Tile(key, "w", size, issue_w(src_fn, shape)))

    def plan_in(key, src_fn, nrows):
        def issue(t):
            dst = ring_f32(t, D)[0:nrows, :]
            S.add("sp", lambda e: e.dma_start(out=dst, in_=src_fn()), w=t.keys, dma=True)
        R.plan.append(Tile(key, "in", (D * 4) // UNIT, issue))

    def plan_out(key):
        R.plan.append(Tile(key, "out", (D * 4) // UNIT, None))

    def wview(w2d):
        return w2d.rearrange("(kc p) m -> p kc m", p=128)

    def row_blocks(n):
        return [(r0, min(128, n - r0)) for r0 in range(0, n, 128)]

    def in_blocks(p):
        base = 0 if p == 0 else PCS[0]
        blks = []
        for r0, nr in row_blocks(PCS[p]):
            blks.append((lambda r0=r0, nr=nr: xp[base + r0:base + r0 + nr, :], nr, r0))
        blks.append((lambda: xs[NSP * p:NSP * (p + 1), :], NSP, PCS[p]))
        return blks

    def out_blocks(p):
        tok0 = 0 if p == 0 else NTOK[0]
        col0 = HALO if p == 0 else 0
        blks = []
        for r0, nr in row_blocks(NTOK[p]):
            blks.append((yp[tok0 + r0:tok0 + r0 + nr, :], nr, col0 + r0))
        blks.append((ys[NSP * p:NSP * (p + 1), :], NSP, PCS[p]))
        return blks

    for p in range(NPASS):
        for bi, (src_fn, nr, c0) in enumerate(in_blocks(p)):
            plan_in(("xin", p, bi), src_fn, nr)
        for i in range(DEPTH):
            j = i // 2
            if i % 2 == 0:
                plan_in(("stp", p, j),
                        lambda j=j, p=p: stp[j, 8 * p:8 * p + 8, :, :].rearrange("b r d -> (b r) d"), 120)
                for g in range(4):
                    plan_w(("pool", p, j, g), lambda j=j, g=g: wview(w_pool[j, g, :, :]), (8, 1024))
            else:
                plan_in(("stc", p, j),
                        lambda j=j, p=p: stc[j, 8 * p:8 * p + 8, :, :].rearrange("b r d -> (b r) d"), 16)
                for cb in range(8):
                    for s3 in (2, 1, 0):
                        for hf in range(2):
                            r0 = (((j * 3 + s3) * 8 + cb) * 2 + hf) * 128
                            plan_w(("cin", p, j, cb, s3, hf),
                                   lambda r0=r0: w_cin[r0:r0 + 128, :].rearrange("p (a b) -> p a b", a=4), (4, 2048))
                    for dq in range(4):
                        r0 = ((j * 8 + cb) * 4 + dq) * 128
                        plan_w(("cout", p, j, cb, dq),
                               lambda r0=r0: w_cout[r0:r0 + 128, :].rearrange("p (a b) -> p a b", a=2), (2, 2048))
            for blk in range(32):
                for hf in range(2):
                    r0 = ((i * 32 + blk) * 2 + hf) * 128
                    plan_w(("up", p, i, blk, hf),
                           lambda r0=r0: w_up[r0:r0 + 128, :].rearrange("p (a b) -> p a b", a=4), (4, 2048))
                for dq in range(4):
                    r0 = ((i * 32 + blk) * 4 + dq) * 128
                    plan_w(("down", p, i, blk, dq),
                           lambda r0=r0: w_down[r0:r0 + 128, :].rearrange("p (a b) -> p a b", a=2), (2, 2048))
        for bi in range(len(out_blocks(p))):
            plan_out(("yout", p, bi))

    def mm_group(s, lhs_fn, rhs_fn, nk, rkeys_fn, ks=None, lo=0):
        spl = [(max(c0, lo), c1) for (c0, c1) in P.SPL if c1 > lo]
        for k in (range(nk) if ks is None else ks):
            lhsT = lhs_fn(k)
            rk = rkeys_fn(k)
            for (c0, c1) in spl:
                rhs = rhs_fn(k, c0, c1)
                S.add("pe",
                      lambda e, lhsT=lhsT, rhs=rhs, c0=c0, c1=c1, k=k: e.matmul(
                          ps(s)[:, c0:c1], lhsT=lhsT, rhs=rhs, start=(k == 0), stop=(k == nk - 1)),
                      r=rk, w=[("ps", s)])

    def norm_stats(need_r2):
        s = next_ps()
        for c in range(NCH):
            b = c % 2
            S.add("act", lambda e, c=c, b=b: e.activation(out=sq_t[:, b, 0:P.T], in_=x_t[:, c, 0:P.T], func=AF.Square),
                  r=[("x", c)], w=[("sq", b)])
            for (c0, c1) in P.SPL:
                S.add("pe", lambda e, c=c, b=b, c0=c0, c1=c1: e.matmul(
                    ps(s)[:, c0:c1], lhsT=ones_t, rhs=sq_t[:, b, c0:c1], start=(c == 0), stop=(c == NCH - 1)),
                    r=[("sq", b), ("c", "ones")], w=[("ps", s)])
        S.add("act", lambda e: e.activation(out=rstd_f[:, 0:P.T], in_=ps(s)[:, 0:P.T], func=AF.Sqrt, scale=1.0 / D,
                                            bias=eps_t[:, 0:1]),
              r=[("c", "eps")], w=[("ps", s), ("rstd",)])
        S.add("dve", lambda e: e.reciprocal(out=rstd_f[:, 0:P.T], in_=rstd_f[:, 0:P.T]), w=[("rstd",)])
        if need_r2:
            S.add("act", lambda e: e.activation(out=rstd2_f[:, 0:P.T], in_=rstd_f[:, 0:P.T], func=AF.Square),
                  r=[("rstd",)], w=[("rstd2",)])

    def make_u(g_t, gi, gname):
        for c in range(NCH):
            S.add("dve", lambda e, c=c: e.tensor_scalar(out=u_t[:, c, 0:P.T], in0=x_t[:, c, 0:P.T],
                                                        scalar1=g_t[:, gi * NCH + c:gi * NCH + c + 1],
                                                        scalar2=None, op0=ALU.mult),
                  r=[("x", c), ("c", gname)], w=[("u", c)])

    def half_tiles(key_fn):
        for hf in range(2):
            t = R.next(key_fn(hf))
            v = ring_bf(t, 32 * 256).rearrange("p (k m) -> p k m", k=32)
            for j2 in range(2):
                yield hf * 2 + j2, j2, v, t
            R.free(t)

    def down_phase(key_fn, hb, lo=0):
        for dh in range(4):
            Dt = R.next(key_fn(dh))
            dv = ring_bf(Dt, 4 * 1024).rearrange("p (j m) -> p j m", j=4)
            def grp(s, mm, ks):
                mm_group(s,
                         lambda k: dv[:, k, mm * 128:(mm + 1) * 128],
                         lambda k, c0, c1: h_t[:, hb, k, c0:c1],
                         4,
                         lambda k: Dt.keys + (("h", hb, k),), ks=ks, lo=lo)

            def evac(s, m):
                S.add("dve", lambda e: e.tensor_tensor(out=x_t[:, m, lo:P.T], in0=ps(s)[:, lo:P.T],
                                                       in1=x_t[:, m, lo:P.T], op=ALU.add),
                      w=[("ps", s), ("x", m)])

            mm = 0
            while mm < 8:
                if dh == 0 and mm == 0:
                    ss = [next_ps(), next_ps(), next_ps()]
                    for q in range(3):
                        grp(ss[q], q, (0, 1, 2))
                    for q in range(3):
                        grp(ss[q], q, (3,))
                    for q in range(3):
                        evac(ss[q], q)
                    mm = 3
                else:
                    s = next_ps()
                    grp(s, mm, None)
                    evac(s, dh * 8 + mm)
                    mm += 1
            R.free(Dt)

    blk_ctr = [0]

    FFN_LO = (14, 16, 32, 34)

    def ffn(p, i):
        norm_stats(False)
        make_u(gmlp_t, i, "gmlp")
        lo = FFN_LO[i] if p == 0 else 0
        for blk in range(32):
            hb = 0
            for jj, j2, uv, U in half_tiles(lambda hf: ("up", p, i, blk, hf)):
                s = next_ps()
                mm_group(s,
                         lambda k: uv[:, k, j2 * 128:(j2 + 1) * 128],
                         lambda k, c0, c1: u_t[:, k, c0:c1],
                         32,
                         lambda k: U.keys + (("u", k),), lo=lo)
                sl = scr_ctr[0] % 5
                scr_ctr[0] += 1
                S.add("dve", lambda e, s=s, sl=sl: e.scalar_tensor_tensor(
                    out=scr_t[:, sl, PAD + lo:PAD + P.T], in0=ps(s)[:, lo:P.T], scalar=0.0, in1=rstd_f[:, lo:P.T],
                    op0=ALU.max, op1=ALU.mult),
                    r=[("rstd",)], w=[("ps", s), ("scr", sl)])
                S.add("act", lambda e, sl=sl, hb=hb, jj=jj: e.activation(
                    out=h_t[:, hb, jj, lo:P.T], in_=scr_t[:, sl, PAD + lo:PAD + P.T], func=AF.Square),
                    r=[("scr", sl)], w=[("h", hb, jj)])
            down_phase(lambda dh: ("down", p, i, blk, dh), hb, lo=lo)

    def conv_layer(p, i):
        j = i // 2
        norm_stats(True)
        make_u(gmix_t, i, "gmix")
        st = R.next(("stc", p, j))
        stv = ring_f32(st, D)
        for cg in range(8):
            s = next_ps()
            for q in range(4):
                c = cg * 4 + q
                o_ = ps(s)[:, q * 16:(q + 1) * 16]
                i_ = stv[0:16, c * 128:(c + 1) * 128]
                S.add("pe", lambda e, o_=o_, i_=i_: e.transpose(out=o_, in_=i_, identity=idn_t[0:16, 0:16]),
                      r=st.keys + (("c", "idn"),), w=[("ps", s)])
            o_ = vst_t[:, cg * 4:cg * 4 + 4, :]
            i_ = ps(s)[:, 0:64].rearrange("p (q r) -> p q r", q=4)
            S.add("act", lambda e, o_=o_, i_=i_: e.copy(out=o_, in_=i_), w=[("ps", s)] + TLK)
        R.free(st)
        cwb = j * 3 * NCH

        def wk(kk, c):
            return cw_t[:, cwb + kk * NCH + c:cwb + kk * NCH + c + 1]

        for cb in range(8):
            hb = 0
            for jj, j2, hv, Ht in half_tiles(lambda hf: ("cin", p, j, cb, 2, hf)):
                s = next_ps()
                mm_group(s, lambda k: hv[:, k, j2 * 128:(j2 + 1) * 128],
                         lambda k, c0, c1: u_t[:, k, c0:c1], 32, lambda k: Ht.keys + (("u", k),))
                S.add("dve", lambda e, s=s, jj=jj: e.tensor_tensor(
                    out=scr_t[:, jj, PAD:PAD + P.T], in0=ps(s)[:, 0:P.T], in1=rstd2_f[:, 0:P.T], op=ALU.mult),
                    r=[("rstd2",)], w=[("ps", s), ("scr", jj)])
            cvs = (4, 0, 1, 2)
            for jj, j2, cv_, Ct in half_tiles(lambda hf: ("cin", p, j, cb, 1, hf)):
                c = cb * 4 + jj
                s = next_ps()
                mm_group(s, lambda k: cv_[:, k, j2 * 128:(j2 + 1) * 128],
                         lambda k, c0, c1: u_t[:, k, c0:c1], 32, lambda k: Ct.keys + (("u", k),))
                S.add("dve", lambda e, s=s, jj=jj: e.tensor_tensor(
                    out=scr_t[:, jj, PAD:PAD + P.T], in0=ps(s)[:, 0:P.T], in1=scr_t[:, jj, PAD:PAD + P.T], op=ALU.mult),
                    w=[("ps", s), ("scr", jj)])
                V = scr_t[:, jj, :]
                cs = cvs[jj]
                CV = scr_t[:, cs, :]
                if p == 0:
                    S.add("act", lambda e, V=V, c=c: e.copy(out=csv_t[:, j, c, :], in_=V[:, PAD + P.PC - 2:PAD + P.PC]),
                          r=[("scr", jj)], w=[("csv", j, c)])
                else:
                    S.add("act", lambda e, V=V, c=c: e.copy(out=V[:, PAD - 2:PAD], in_=csv_t[:, j, c, :]),
                          r=[("csv", j, c)], w=[("scr", jj)])
                S.add("act", lambda e, c=c: e.copy(out=vtl_t[:, :, 0:2],
                                                   in_=vst_t[:, c, :].rearrange("p (b r) -> p b r", b=8)),
                      r=TLK, w=[("vtl",)])
                S.add("act", lambda e, V=V: e.copy(out=vtl_t[:, :, 2:6],
                                                   in_=V[:, PAD + P.PC:PAD + P.T].rearrange("p (b r) -> p b r", b=8)),
                      r=[("scr", jj)], w=[("vtl",)])
                vb_ = c % 2
                S.add("act", lambda e, V=V, vb_=vb_: e.copy(out=vsv_t[:, vb_, 0:2], in_=V[:, PAD + P.PC - 2:PAD + P.PC]),
                      r=[("scr", jj)], w=[("vsv", vb_)])
                S.add("act", lambda e, V=V, vb_=vb_: e.copy(
                    out=vsv_t[:, vb_, 2:18].rearrange("p (b r) -> p b r", b=8),
                    in_=V[:, PAD + P.PC:PAD + P.T].rearrange("p (b r) -> p b r", b=8)[:, :, 2:4]),
                    r=[("scr", jj)], w=[("vsv", vb_)])
                s3 = next_ps()
                S.add("pe", lambda e, s3=s3, vb_=vb_: e.transpose(out=ps(s3)[0:18, 0:128], in_=vsv_t[:, vb_, :],
                                                                  identity=idn_t),
                      r=[("vsv", vb_), ("c", "idn")], w=[("ps", s3)])
                ob = (c // 2) % 2
                S.add("act", lambda e, s3=s3, ob=ob, c=c: e.copy(out=ost_t[0:18, ob, (c % 2) * 128:(c % 2 + 1) * 128],
                                                                 in_=ps(s3)[0:18, 0:128]),
                      w=[("ps", s3), ("ost", ob)])
                if c % 2 == 1:
                    col = (c - 1) * 128
                    if p == NPASS - 1:
                        S.add("sp", lambda e, ob=ob, col=col: e.dma_start(out=conv_p[j, :, col:col + 256],
                                                                           in_=ost_t[0:2, ob, :]),
                              r=[("ost", ob)], dma=True)
                    S.add("sp", lambda e, ob=ob, col=col: e.dma_start(
                        out=conv_s[j, 16 * p:16 * p + 16, col:col + 256], in_=ost_t[2:18, ob, :]),
                        r=[("ost", ob)], dma=True)
                S.add("act", lambda e, V=V, CV=CV, c=c: e.activation(
                    out=CV[:, PAD:PAD + P.PC], in_=V[:, PAD - 2:PAD + P.PC - 2], func=AF.Copy, scale=wk(0, c)),
                    r=[("scr", jj), ("c", "cw")], w=[("scr", cs)])
                S.add("dve", lambda e, V=V, CV=CV, c=c: e.scalar_tensor_tensor(
                    out=CV[:, PAD:PAD + P.PC], in0=V[:, PAD - 1:PAD + P.PC - 1], scalar=wk(1, c), in1=CV[:, PAD:PAD + P.PC],
                    op0=ALU.mult, op1=ALU.add),
                    r=[("scr", jj), ("c", "cw")], w=[("scr", cs)])
                S.add("dve", lambda e, V=V, CV=CV, c=c: e.scalar_tensor_tensor(
                    out=CV[:, PAD:PAD + P.PC], in0=V[:, PAD:PAD + P.PC], scalar=wk(2, c), in1=CV[:, PAD:PAD + P.PC],
                    op0=ALU.mult, op1=ALU.add),
                    r=[("scr", jj), ("c", "cw")], w=[("scr", cs)])
                CVs = CV[:, PAD + P.PC:PAD + P.T].rearrange("p (b r) -> p b r", b=8)
                S.add("dve", lambda e, CVs=CVs, c=c: e.tensor_scalar(
                    out=CVs, in0=vtl_t[:, :, 0:4], scalar1=wk(0, c), scalar2=None, op0=ALU.mult),
                    r=[("vtl",), ("c", "cw")], w=[("scr", cs)])
                S.add("dve", lambda e, CVs=CVs, c=c: e.scalar_tensor_tensor(
                    out=CVs, in0=vtl_t[:, :, 1:5], scalar=wk(1, c), in1=CVs, op0=ALU.mult, op1=ALU.add),
                    r=[("vtl",), ("c", "cw")], w=[("scr", cs)])
                S.add("dve", lambda e, CVs=CVs, c=c: e.scalar_tensor_tensor(
                    out=CVs, in0=vtl_t[:, :, 2:6], scalar=wk(2, c), in1=CVs, op0=ALU.mult, op1=ALU.add),
                    r=[("vtl",), ("c", "cw")], w=[("scr", cs)])
                S.add("dve", lambda e, CV=CV: e.tensor_tensor(out=CV[:, PAD:PAD + P.T], in0=CV[:, PAD:PAD + P.T],
                                                              in1=rstd_f[:, 0:P.T], op=ALU.mult),
                      r=[("rstd",)], w=[("scr", cs)])
            for jj, j2, bv, Bt in half_tiles(lambda hf: ("cin", p, j, cb, 0, hf)):
                s = next_ps()
                cs = cvs[jj]
                mm_group(s, lambda k: bv[:, k, j2 * 128:(j2 + 1) * 128],
                         lambda k, c0, c1: u_t[:, k, c0:c1], 32, lambda k: Bt.keys + (("u", k),))
                S.add("dve", lambda e, s=s, cs=cs, hb=hb, jj=jj: e.tensor_tensor(
                    out=h_t[:, hb, jj, 0:P.T], in0=ps(s)[:, 0:P.T], in1=scr_t[:, cs, PAD:PAD + P.T], op=ALU.mult),
                    r=[("scr", cs)], w=[("ps", s), ("h", hb, jj)])
            down_phase(lambda dh: ("cout", p, j, cb, dh), hb)

    def pool_layer(p, i):
        j = i // 2
        S.add("sp", lambda e: e.dma_start(out=pos1_f[:, 0:P.T], in_=pos[p:p + 1, 0:P.T].partition_broadcast(128)),
              w=[("rstd2",)], dma=True)
        S.add("dve", lambda e: e.tensor_scalar(out=pos1_f[:, 0:P.T], in0=pos1_f[:, 0:P.T], scalar1=1.0, scalar2=1.0,
                                               op0=ALU.add, op1=ALU.max), w=[("rstd2",)])
        norm_stats(False)
        st = R.next(("stp", p, j))
        stv = ring_f32(st, D)
        if p == 0:
            S.add("sp", lambda e: e.dma_start(out=pool_so[j, :, :, :], in_=stp[j, :, 4:15, :]), dma=True)
        A, B_, C_, Dn = 0, 1, 2, 3
        for g in range(4):
            w = WINS[g]
            S.add("dve", lambda e, w=w: e.tensor_scalar(out=scr_t[:, Dn, PAD:PAD + P.PC], in0=pos1_f[:, 0:P.PC],
                                                        scalar1=float(w), scalar2=None, op0=ALU.min),
                  r=[("rstd2",)], w=[("scr", Dn)])
            S.add("dve", lambda e: e.reciprocal(out=scr_t[:, Dn, PAD:PAD + P.PC], in_=scr_t[:, Dn, PAD:PAD + P.PC]),
                  w=[("scr", Dn)])
            for cc in range(8):
                c = g * 8 + cc
                gcol = i * NCH + c
                S.add("dve", lambda e, c=c, gcol=gcol: e.scalar_tensor_tensor(
                    out=scr_t[:, A, PAD:PAD + P.T], in0=x_t[:, c, 0:P.T], scalar=gmix_t[:, gcol:gcol + 1], in1=rstd_f[:, 0:P.T],
                    op0=ALU.mult, op1=ALU.mult),
                    r=[("x", c), ("rstd",), ("c", "gmix")], w=[("scr", A)])
                s2 = next_ps()
                S.add("pe", lambda e, s2=s2, c=c: e.transpose(out=ps(s2)[:, 0:120],
                                                               in_=stv[0:120, c * 128:(c + 1) * 128],
                                                               identity=idn_t[0:120, 0:120]),
                      r=st.keys + (("c", "idn"),), w=[("ps", s2)])
                S.add("act", lambda e, s2=s2: e.copy(out=tl_t[:, 0, :, 0:15],
                                                     in_=ps(s2)[:, 0:120].rearrange("p (b r) -> p b r", b=8)),
                      w=[("ps", s2), ("tl", 0)])
                S.add("act", lambda e: e.copy(out=tl_t[:, 0, :, 15:19],
                                              in_=scr_t[:, A, PAD + P.PC:PAD + P.T].rearrange("p (b r) -> p b r", b=8)),
                      r=[("scr", A)], w=[("tl", 0)])
                s3 = next_ps()
                S.add("pe", lambda e, s3=s3: e.transpose(out=ps(s3)[0:47, 0:128], in_=scr_t[:, A, PAD + P.PC - 15:PAD + P.T],
                                                         identity=idn_t),
                      r=[("scr", A), ("c", "idn")], w=[("ps", s3)])
                ob = (c // 2) % 2
                S.add("act", lambda e, s3=s3, ob=ob, c=c: e.copy(out=ost_t[0:47, ob, (c % 2) * 128:(c % 2 + 1) * 128],
                                                                 in_=ps(s3)[0:47, 0:128]),
                      w=[("ps", s3), ("ost", ob)])
                if c % 2 == 1:
                    col = (c - 1) * 128
                    if p == NPASS - 1:
                        S.add("sp", lambda e, ob=ob, col=col: e.dma_start(out=pool_p[j, :, col:col + 256],
                                                                           in_=ost_t[0:15, ob, :]),
                              r=[("ost", ob)], dma=True)
                    S.add("sp", lambda e, ob=ob, col=col: e.dma_start(
                        out=pool_sn[j, NSP * p:NSP * (p + 1), col:col + 256], in_=ost_t[15:47, ob, :]),
                        r=[("ost", ob)], dma=True)
                if p == 0:
                    S.add("act", lambda e, c=c: e.copy(out=psv_t[:, j, c, :],
                                                       in_=scr_t[:, A, PAD + P.PC - 15:PAD + P.PC]),
                          r=[("scr", A)], w=[("psv", j, c)])
                else:
                    S.add("act", lambda e, c=c: e.copy(out=scr_t[:, A, PAD - 15:PAD], in_=psv_t[:, j, c, :]),
                          r=[("psv", j, c)], w=[("scr", A)])
                cur = A
                nxt_slots = [B_, C_]
                for si in range(g + 1):
                    sft = 1 << si
                    lo = PAD - 15 + 2 * sft - 1
                    o_ = nxt_slots[si % 2]
                    S.add("dve", lambda e, cur=cur, o_=o_, sft=sft, lo=lo: e.tensor_tensor(
                        out=scr_t[:, o_, lo:PAD + P.PC], in0=scr_t[:, cur, lo:PAD + P.PC],
                        in1=scr_t[:, cur, lo - sft:PAD + P.PC - sft], op=ALU.add),
                        r=[("scr", cur)], w=[("scr", o_)])
                    cur = o_
                if p == 0:
                    o_ = B_ if cur == C_ else C_
                    S.add("dve", lambda e, cur=cur, o_=o_: e.tensor_tensor(
                        out=scr_t[:, o_, PAD:PAD + P.PC], in0=scr_t[:, cur, PAD:PAD + P.PC],
                        in1=scr_t[:, Dn, PAD:PAD + P.PC], op=ALU.mult),
                        r=[("scr", cur), ("scr", Dn)], w=[("scr", o_)])
                    S.add("dve", lambda e, o_=o_, c=c: e.tensor_tensor(
                        out=u_t[:, c, 0:P.PC], in0=scr_t[:, o_, PAD:PAD + P.PC], in1=scr_t[:, A, PAD:PAD + P.PC],
                        op=ALU.subtract),
                        r=[("scr", o_), ("scr", A)], w=[("u", c)])
                else:
                    S.add("dve", lambda e, cur=cur, c=c, w=w: e.scalar_tensor_tensor(
                        out=u_t[:, c, 0:P.PC], in0=scr_t[:, cur, PAD:PAD + P.PC], scalar=1.0 / w,
                        in1=scr_t[:, A, PAD:PAD + P.PC], op0=ALU.mult, op1=ALU.subtract),
                        r=[("scr", cur), ("scr", A)], w=[("u", c)])
                tc = 0
                for si in range(g + 1):
                    sft = 1 << si
                    lo = 2 * sft - 1
                    to = 1 if tc != 1 else 2
                    S.add("dve", lambda e, tc=tc, to=to, sft=sft, lo=lo: e.tensor_tensor(
                        out=tl_t[:, to, :, lo:19], in0=tl_t[:, tc, :, lo:19], in1=tl_t[:, tc, :, lo - sft:19 - sft],
                        op=ALU.add),
                        r=[("tl", tc)], w=[("tl", to)])
                    tc = to
                S.add("dve", lambda e, tc=tc, c=c, w=w: e.scalar_tensor_tensor(
                    out=u_t[:, c, P.PC:P.T].rearrange("p (b r) -> p b r", b=8), in0=tl_t[:, tc, :, 15:19],
                    scalar=1.0 / w, in1=tl_t[:, 0, :, 15:19], op0=ALU.mult, op1=ALU.subtract),
                    r=[("tl", tc), ("tl", 0)], w=[("u", c)])
            if g == 3:
                R.free(st)
            Wg = R.next(("pool", p, j, g))
            wv = ring_bf(Wg, 8 * 1024).rearrange("p (k m) -> p k m", k=8)
            for mm in range(8):
                m = g * 8 + mm
                s = next_ps()
                mm_group(s, lambda k, mm=mm: wv[:, k, mm * 128:(mm + 1) * 128],
                         lambda k, c0, c1: u_t[:, g * 8 + k, c0:c1], 8,
                         lambda k: Wg.keys + (("u", g * 8 + k),))
                S.add("dve", lambda e, s=s, m=m: e.scalar_tensor_tensor(
                    out=x_t[:, m, 0:P.T], in0=ps(s)[:, 0:P.T], scalar=psc_t[:, j * NCH + m:j * NCH + m + 1],
                    in1=x_t[:, m, 0:P.T], op0=ALU.mult, op1=ALU.add),
                    r=[("c", "psc")], w=[("ps", s), ("x", m)])
            R.free(Wg)

    def load_x(p):
        for bi, (src_fn, nr, c0) in enumerate(in_blocks(p)):
            t = R.next(("xin", p, bi))
            tv = ring_f32(t, D)
            for cg in range(8):
                s = next_ps()
                for q in range(4):
                    c = cg * 4 + q
                    o_ = ps(s)[:, q * 128:q * 128 + nr]
                    i_ = tv[0:nr, c * 128:(c + 1) * 128]
                    id_ = idn_t[0:nr, 0:nr]
                    S.add("pe", lambda e, o_=o_, i_=i_, id_=id_: e.transpose(out=o_, in_=i_, identity=id_),
                          r=t.keys + (("c", "idn"),), w=[("ps", s)])
                eng = "act" if cg % 2 else "dve"
                src = ps(s)[:, 0:512].rearrange("p (q n) -> p q n", q=4)[:, :, 0:nr]
                dst = x_t[:, cg * 4:cg * 4 + 4, c0:c0 + nr]
                if eng == "act":
                    S.add("act", lambda e, src=src, dst=dst: e.copy(out=dst, in_=src),
                          w=[("ps", s)] + [("x", cg * 4 + q) for q in range(4)])
                else:
                    S.add("dve", lambda e, src=src, dst=dst: e.tensor_copy(out=dst, in_=src),
                          w=[("ps", s)] + [("x", cg * 4 + q) for q in range(4)])
            R.free(t)

    def final_out(p):
        norm_stats(False)
        for c in range(NCH):
            S.add("dve", lambda e, c=c: e.scalar_tensor_tensor(
                out=x_t[:, c, 0:P.T], in0=x_t[:, c, 0:P.T], scalar=gfin_t[:, c:c + 1], in1=rstd_f[:, 0:P.T],
                op0=ALU.mult, op1=ALU.mult),
                r=[("rstd",), ("c", "gfin")], w=[("x", c)])
        for bi, (dst_ap, n, c0) in enumerate(out_blocks(p)):
            t = R.next(("yout", p, bi))
            tv = ring_f32(t, D)
            for cg in range(8):
                s = next_ps()
                for q in range(4):
                    c = cg * 4 + q
                    o_ = ps(s)[0:n, q * 128:(q + 1) * 128]
                    i_ = x_t[:, c, c0:c0 + n]
                    S.add("pe", lambda e, o_=o_, i_=i_: e.transpose(out=o_, in_=i_, identity=idn_t),
                          r=[("x", c), ("c", "idn")], w=[("ps", s)])
                eng = "act" if cg % 2 else "dve"
                src = ps(s)[0:n, 0:512]
                dst = tv[0:n, cg * 512:(cg + 1) * 512]
                if eng == "act":
                    S.add("act", lambda e, src=src, dst=dst: e.copy(out=dst, in_=src), w=[("ps", s)] + list(t.keys))
                else:
                    S.add("dve", lambda e, src=src, dst=dst: e.tensor_copy(out=dst, in_=src),
                          w=[("ps", s)] + list(t.keys))
            i_ = tv[0:n, :]
            S.add("sp", lambda e, dst_ap=dst_ap, i_=i_: e.dma_start(out=dst_ap, in_=i_), r=t.keys, dma=True)
            R.free(t)

    R.pump()
    for p in range(NPASS):
        P.T, P.PC, P.SPL = TS[p], PCS[p], SPLS[p]
        load_x(p)
        for i in range(DEPTH):
            if i % 2 == 0:
                pool_layer(p, i)
            else:
                conv_layer(p, i)
            ffn(p, i)
        final_out(p)
    assert R.cons == len(R.plan)

    S.finalize()

    sem_ctxs = {}
    sems = {}
    for name in ("pe", "act", "dve"):
        c = nc.semaphore("s_" + name)
        sem_ctxs[name] = c
        sems[name] = c.__enter__()
    for q, n in (("pool", N_WSEM), ("sp", N_IOSEM)):
        for i in range(n):
            c = nc.semaphore("s_%s%d" % (q, i))
            sem_ctxs[(q, i)] = c
            sems[(q, i)] = c.__enter__()
    with nc.Block() as block:
        S.emit(nc, block, sems)
    for c in reversed(list(sem_ctxs.values())):
        c.__exit__(None, None, None)
    ctx_ps.__exit__(None, None, None)
    ctx_arena.__exit__(None, None, None)
    return nc


_NC_CACHE = {}


def _layout_vec(v):
    v = np.asarray(v, dtype=np.float32)
    if v.ndim == 1:
        v = v[None, :]
    n = v.shape[0]
    return np.ascontiguousarray(v.reshape(n, NCH, 128).transpose(2, 0, 1).reshape(128, n * NCH))


def kernel(x_prompt, x_sample, state_pool, state_conv, norm_mix, norm_mlp, norm_final,
           w_pool, pool_scale, w_conv_in, conv_w, w_conv_out, w_up, w_down):
    n = 8
    x_prompt = np.asarray(x_prompt, dtype=np.float32)
    x_sample = np.asarray(x_sample, dtype=np.float32)
    state_pool = np.asarray(state_pool, dtype=np.float32)
    state_conv = np.asarray(state_conv, dtype=np.float32)
    if "nc" not in _NC_CACHE:
        _NC_CACHE["nc"] = build_nc()
    nc = _NC_CACHE["nc"]

    gmix = _layout_vec(norm_mix)
    gmlp = _layout_vec(norm_mlp)
    gfin = _layout_vec(norm_final)
    psc = _layout_vec(pool_scale)
    cw = _layout_vec(np.asarray(conv_w, dtype=np.float32).reshape(6, D))
    ident = np.eye(128, dtype=np.float32)
    def up_like(w, nb):
        L = w.shape[0]
        v = np.asarray(w, dtype=np.float32).reshape(L, 32, 128, nb, 2, 256)
        return np.ascontiguousarray(v.transpose(0, 3, 4, 2, 1, 5)).reshape(L * nb * 2 * 128, 32 * 256)

    def down_like(w, nb):
        L = w.shape[0]
        v = np.asarray(w, dtype=np.float32).reshape(L, nb, 4, 128, 4, 1024)
        return np.ascontiguousarray(v.transpose(0, 1, 4, 3, 2, 5)).reshape(L * nb * 4 * 128, 4 * 1024)

    wci = np.asarray(w_conv_in, dtype=np.float32).reshape(2, D, 3, D).transpose(0, 2, 1, 3).reshape(6, D, D)
    weights = {
        "w_pool": np.asarray(w_pool, dtype=np.float32),
        "w_conv_in": up_like(wci, 8),
        "w_conv_out": down_like(w_conv_out, 8),
        "w_up": up_like(w_up, 32),
        "w_down": down_like(w_down, 32),
    }
    del wci
    in_maps = []
    for c in range(n):
        seq, half = c // 2, c % 2
        xp = np.zeros((HALO + 1024, D), np.float32)
        lo = half * 1024 - HALO
        if lo < 0:
            xp[HALO:] = x_prompt[seq, 0:1024]
        else:
            xp[:] = x_prompt[seq, lo:lo + HALO + 1024]
        pos = np.zeros((NPASS, TMAX), np.float32)
        pos[0, 0:PCS[0]] = half * 1024 - HALO + np.arange(PCS[0])
        pos[1, 0:PCS[1]] = half * 1024 + NTOK[0] + np.arange(PCS[1])
        for p in range(NPASS):
            pos[p, PCS[p]:TS[p]] = 16384 + np.tile(np.arange(4), 8)
        m = {
            "xp": xp,
            "xs": np.ascontiguousarray(x_sample[16 * c:16 * c + 16].reshape(64, D)),
            "stp": np.ascontiguousarray(state_pool[:, 16 * c:16 * c + 16]),
            "stc": np.ascontiguousarray(state_conv[:, 16 * c:16 * c + 16]),
            "pos": pos,
            "gmix": gmix, "gmlp": gmlp, "gfin": gfin, "pscale": psc, "convw": cw, "ident": ident,
        }
        m.update(weights)
        in_maps.append(m)

    res = run_bass_kernel_spmd(nc, in_maps, core_ids=list(range(n)))
    outs = res.results

    y_prompt = np.empty((4, 2048, D), np.float32)
    y_sample = np.empty((128, 4, D), np.float32)
    new_pool_prompt = np.empty((2, 4, 15, D), np.float32)
    new_pool_sample = np.empty((2, 128, 15, D), np.float32)
    new_conv_prompt = np.empty((2, 4, 2, D), np.float32)
    new_conv_sample = np.empty((2, 128, 2, D), np.float32)
    for c in range(n):
        seq, half = c // 2, c % 2
        o = outs[c]
        y_prompt[seq, half * 1024:(half + 1) * 1024] = o["yp"]
        y_sample[16 * c:16 * c + 16] = o["ys"].reshape(16, 4, D)
        new_pool_sample[:, 16 * c:16 * c + 16, 0:11] = o["pool_so"]
        new_pool_sample[:, 16 * c:16 * c + 16, 11:15] = o["pool_sn"].reshape(2, 16, 4, D)
        new_conv_sample[:, 16 * c:16 * c + 16] = o["conv_s"].reshape(2, 16, 2, D)
        if half == 1:
            new_pool_prompt[:, seq] = o["pool_p"]
            new_conv_prompt[:, seq] = o["conv_p"]
    return (y_prompt, y_sample, new_pool_prompt, new_pool_sample, new_conv_prompt, new_conv_sample)
```
